# Optimizing a Trainium2 kernel written in Bass

```python
import jax, jax.numpy as jnp
from jax import lax
import numpy as np

D_MODEL = 1024
BATCH = 4
SEQ = 8192
DEPTH = 2

ATTN_WIDTH = D_MODEL // 2
HEAD_DIM = 64
N_ATTN_HEADS = ATTN_WIDTH // HEAD_DIM
CONV_WIDTH = D_MODEL - ATTN_WIDTH
CONV_K = 3
DILATED_BRANCHES = ((128, 1), (512, 4), (2048, 16))
BLOCK = 128
D_FF = 2816
N_SUB = 3
N_MOD = 3
IN_COLS = 3 * ATTN_WIDTH + 3 * CONV_WIDTH
EPS = 1e-6
NEG = -1e30

kernel_name = "hymba_dilated_attn_shortconv_macaron"


def _rmsnorm(x, g):
    xf = x.astype(jnp.float32)
    y = xf * lax.rsqrt(jnp.mean(xf * xf, axis=-1, keepdims=True) + EPS)
    return (y * g.astype(jnp.float32)).astype(x.dtype)


def _band_attention(q, k, v, span):
    n, L, h, hd = q.shape
    nb = L // BLOCK
    qb = q.reshape(n, nb, BLOCK, h, hd)
    kb = k.reshape(n, nb, BLOCK, h, hd)
    vb = v.reshape(n, nb, BLOCK, h, hd)
    zk = jnp.zeros_like(kb[:, :1])
    k2 = jnp.concatenate([jnp.concatenate([zk, kb[:, :-1]], axis=1), kb], axis=2)
    v2 = jnp.concatenate([jnp.concatenate([zk, vb[:, :-1]], axis=1), vb], axis=2)
    s = jnp.einsum('nbqhd,nbkhd->nbhqk', qb, k2).astype(jnp.float32) * (hd ** -0.5)
    qi = jnp.arange(BLOCK)[:, None] + BLOCK
    kj = jnp.arange(2 * BLOCK)[None, :]
    dist = qi - kj
    band = (dist >= 0) & (dist <= span)
    blk = jnp.arange(nb)[:, None, None]
    mask = band[None] & ((blk > 0) | (kj[None] >= BLOCK))
    s = jnp.where(mask[None, :, None], s, NEG)
    m = jnp.max(s, axis=-1, keepdims=True)
    p = jnp.exp(s - m)
    l = jnp.sum(p, axis=-1, keepdims=True)
    o = jnp.einsum('nbhqk,nbkhd->nbqhd', (p / l).astype(v.dtype), v2)
    lse = (m + jnp.log(l))[..., 0]
    return o.reshape(n, L, h, hd), lse.transpose(0, 1, 3, 2).reshape(n, L, h)


def _dilated_branch(q, k, v, window, dilation):
    b, s, h, hd = q.shape
    L = s // dilation
    Lp = -(-L // BLOCK) * BLOCK

    def to_res(t):
        t = t.reshape(b, L, dilation, h, hd).transpose(0, 2, 1, 3, 4).reshape(b * dilation, L, h, hd)
        return jnp.pad(t, ((0, 0), (0, Lp - L), (0, 0), (0, 0)))

    o, lse = _band_attention(to_res(q), to_res(k), to_res(v), window // dilation)
    o = o[:, :L].reshape(b, dilation, L, h, hd).transpose(0, 2, 1, 3, 4).reshape(b, s, h, hd)
    lse = lse[:, :L].reshape(b, dilation, L, h).transpose(0, 2, 1, 3).reshape(b, s, h)
    return o, lse


def _dilated_attention(q, k, v):
    outs, lses = [], []
    for window, dilation in DILATED_BRANCHES:
        o, lse = _dilated_branch(q, k, v, window, dilation)
        outs.append(o)
        lses.append(lse)
    wts = jax.nn.softmax(jnp.stack(lses, axis=-1), axis=-1)
    return jnp.einsum('bshr,rbshd->bshd', wts.astype(q.dtype), jnp.stack(outs, axis=0))


def _short_conv(u, w, bias):
    y = lax.conv_general_dilated(
        u, w[:, None, :].astype(u.dtype), window_strides=(1,),
        padding=((CONV_K - 1, 0),), dimension_numbers=('NWC', 'WIO', 'NWC'),
        feature_group_count=u.shape[-1])
    return y + bias


def _swiglu(h, w1, w2):
    g, up = jnp.split(h @ w1, 2, axis=-1)
    return (jax.nn.silu(g) * up) @ w2


def _mixer(h, w_in, q_g, k_g, conv_w, conv_b, w_out):
    b, s, _ = h.shape
    A, C = ATTN_WIDTH, CONV_WIDTH
    proj = h @ w_in
    q, k, v, gb, gc, u = jnp.split(proj, [A, 2 * A, 3 * A, 3 * A + C, 3 * A + 2 * C], axis=-1)
    q = _rmsnorm(q.reshape(b, s, N_ATTN_HEADS, HEAD_DIM), q_g)
    k = _rmsnorm(k.reshape(b, s, N_ATTN_HEADS, HEAD_DIM), k_g)
    v = v.reshape(b, s, N_ATTN_HEADS, HEAD_DIM)
    y_attn = _dilated_attention(q, k, v).reshape(b, s, A)
    y_conv = gb * _short_conv(gc * u, conv_w, conv_b)
    return jnp.concatenate([y_attn, y_conv], axis=-1) @ w_out


def setup_inputs(seed: int = 0) -> dict:
    key = jax.random.key(seed)
    ks = jax.random.split(key, 14)
    D = D_MODEL
    nrm = jax.random.normal
    return {
        "x": nrm(ks[0], (BATCH, SEQ, D), jnp.float32),
        "c": nrm(ks[1], (BATCH, D), jnp.float32),
        "w_ada": nrm(ks[2], (DEPTH, D, N_SUB * N_MOD * D), jnp.float32) * (0.5 * D ** -0.5),
        "b_ada": nrm(ks[3], (DEPTH, N_SUB * N_MOD * D), jnp.float32) * 0.02,
        "norm_g": 1.0 + 0.02 * nrm(ks[4], (DEPTH, N_SUB, D), jnp.float32),
        "w_in": nrm(ks[5], (DEPTH, D, IN_COLS), jnp.float32) * D ** -0.5,
        "q_norm_g": 1.0 + 0.02 * nrm(ks[6], (DEPTH, HEAD_DIM), jnp.float32),
        "k_norm_g": 1.0 + 0.02 * nrm(ks[7], (DEPTH, HEAD_DIM), jnp.float32),
        "conv_w": nrm(ks[8], (DEPTH, CONV_K, CONV_WIDTH), jnp.float32) * CONV_K ** -0.5,
        "conv_b": nrm(ks[9], (DEPTH, CONV_WIDTH), jnp.float32) * 0.02,
        "w_out": nrm(ks[10], (DEPTH, D, D), jnp.float32) * D ** -0.5,
        "ffn_w1": nrm(ks[11], (DEPTH, 2, D, 2 * D_FF), jnp.float32) * D ** -0.5,
        "ffn_w2": nrm(ks[12], (DEPTH, 2, D_FF, D), jnp.float32) * D_FF ** -0.5,
    }


def reference(x, c, w_ada, b_ada, norm_g, w_in, q_norm_g, k_norm_g, conv_w, conv_b,
              w_out, ffn_w1, ffn_w2):
    b = x.shape[0]
    for layer in range(DEPTH):
        mod = (jax.nn.silu(c) @ w_ada[layer] + b_ada[layer]).reshape(b, N_SUB, N_MOD, D_MODEL)
        shift = mod[:, :, 0, None, :]
        scale = mod[:, :, 1, None, :]
        gate = mod[:, :, 2, None, :]

        def ada(z, i):
            return _rmsnorm(z, norm_g[layer, i]) * (1.0 + scale[:, i]) + shift[:, i]

        x = x + 0.5 * gate[:, 0] * _swiglu(ada(x, 0), ffn_w1[layer, 0], ffn_w2[layer, 0])
        x = x + gate[:, 1] * _mixer(ada(x, 1), w_in[layer], q_norm_g[layer], k_norm_g[layer],
                                    conv_w[layer], conv_b[layer], w_out[layer])
        x = x + 0.5 * gate[:, 2] * _swiglu(ada(x, 2), ffn_w1[layer, 1], ffn_w2[layer, 1])
    return x
```

```python
import numpy as np
from contextlib import ExitStack
import concourse.bass as bass
import concourse.mybir as mybir
from concourse.bass_utils import run_bass_kernel_spmd

F32 = mybir.dt.float32
BF16 = mybir.dt.bfloat16
AF = mybir.ActivationFunctionType
ALU = mybir.AluOpType

ENGS = ("pe", "act", "dve", "pool", "sp")
GEN = 20000
NDMASEM = 12

NCORES = 8
DM = 1024
KC = 8
SC = 4096
TF = 512
NTF = SC // TF
TB = 512
NTB = SC // TB
DFF = 2816
JC = 22
NG1 = 11
HALO = 2048
EPS = 1e-6
FCC = 4 * HALO + 4 * HALO + 16
BRANCH_D = (1, 4, 16)


class Buf:
    __slots__ = ("name", "writer", "readers")

    def __init__(self, name=""):
        self.name = name
        self.writer = None
        self.readers = []


class Op:
    __slots__ = ("eng", "fn", "deps", "dma", "idx", "need_sig", "sigval", "sem", "dmaslot",
                 "dmaprev", "own")

    def __init__(self, eng, fn, dma):
        self.own = False
        self.eng = eng
        self.fn = fn
        self.dma = dma
        self.deps = []
        self.need_sig = False
        self.sigval = None
        self.sem = None
        self.dmaslot = None
        self.dmaprev = None


class Sched:
    def __init__(self):
        self.q = {e: [] for e in ENGS}
        self.ndma = {e: 0 for e in ENGS}
        self.fence_deps = {e: [] for e in ENGS}

    def fence(self):
        deps = []
        for e in ENGS:
            q = self.q[e]
            last_c = None
            nd = 0
            for o in reversed(q):
                if o.own:
                    continue
                if o.dma:
                    if nd < NDMASEM + 2:
                        deps.append(o)
                        nd += 1
                elif last_c is None:
                    last_c = o
                    deps.append(o)
                if last_c is not None and nd >= NDMASEM + 2:
                    break
        for e in ENGS:
            self.fence_deps[e] = list(deps)

    def op(self, eng, fn, reads=(), writes=(), dma=False, own_sem=False):
        o = Op(eng, fn, dma)
        o.own = own_sem
        o.idx = len(self.q[eng])
        deps = []
        for b in reads:
            if b.writer is not None:
                deps.append(b.writer)
        for b in writes:
            if b.writer is not None:
                deps.append(b.writer)
            deps.extend(b.readers)
        seen = set()
        for d in deps:
            if d is o or id(d) in seen:
                continue
            seen.add(id(d))
            if d.eng == eng and not d.dma and not dma and not d.own:
                if eng == "pe":
                    continue
                if eng != "pool" and not any(b.writer is d for b in reads):
                    continue
            o.deps.append(d)
            d.need_sig = True
        if self.fence_deps[eng]:
            for d in self.fence_deps[eng]:
                if id(d) in seen:
                    continue
                seen.add(id(d))
                if d.eng == eng and not d.dma and not d.own:
                    continue
                o.deps.append(d)
                d.need_sig = True
            self.fence_deps[eng] = []
        for b in reads:
            b.readers.append(o)
        for b in writes:
            b.writer = o
            b.readers = []
        if dma:
            o.dmaslot = self.ndma[eng] % NDMASEM
            self.ndma[eng] += 1
        self.q[eng].append(o)
        return o

    def finalize(self, nc, stack):
        self.sems = {}

        def getsem(name):
            if name not in self.sems:
                self.sems[name] = stack.enter_context(nc.semaphore(name))
            return self.sems[name]

        for e in ENGS:
            cnt = 0
            gen = 0
            dcnt = [0] * NDMASEM
            dlast = [None] * NDMASEM
            for o in self.q[e]:
                if o.own:
                    o.sem = getsem(f"own_{e}_{o.idx}")
                    o.sigval = 1
                elif o.dma:
                    s = o.dmaslot
                    o.dmaprev = dlast[s]
                    dcnt[s] += 16
                    o.sem = getsem(f"d_{e}_{s}")
                    o.sigval = dcnt[s]
                    dlast[s] = o
                elif o.need_sig:
                    if cnt >= GEN:
                        gen += 1
                        cnt = 0
                    cnt += 1
                    o.sem = getsem(f"c_{e}_{gen}")
                    o.sigval = cnt

    def emit(self, eng, handle):
        waited = {}
        for o in self.q[eng]:
            deps = list(o.deps)
            if o.dma and o.dmaprev is not None:
                deps.append(o.dmaprev)
            for d in deps:
                key = id(d.sem)
                if waited.get(key, 0) >= d.sigval:
                    continue
                handle.wait_ge(d.sem, d.sigval)
                waited[key] = d.sigval
            ins = o.fn(handle)
            if o.own:
                ins.then_inc(o.sem, 1)
            elif o.dma:
                ins.then_inc(o.sem, 16)
            elif o.need_sig:
                ins.then_inc(o.sem, 1)

    def run_block(self, nc, stack, final_waits=()):
        self.finalize(nc, stack)
        S = self
        with nc.Block() as block:
            @block.tensor
            def _(h):
                S.emit("pe", h)

            @block.scalar
            def _(h):
                S.emit("act", h)

            @block.vector
            def _(h):
                S.emit("dve", h)

            @block.gpsimd
            def _(h):
                S.emit("pool", h)

            @block.sync
            def _(h):
                S.emit("sp", h)
                for o in final_waits:
                    h.wait_ge(o.sem, o.sigval)


class Arena:
    cnt = [0]

    def __init__(self, nc, start, limit):
        self.nc = nc
        self.off = start
        self.limit = limit

    def alloc(self, shape, dt, name="t"):
        n = int(np.prod(shape[1:])) * (4 if dt == F32 else 2)
        Arena.cnt[0] += 1
        t = self.nc.alloc_sbuf_tensor_at(f"{name}{Arena.cnt[0]}", list(shape), dt, offset=self.off)
        self.off += (n + 31) // 32 * 32
        assert self.off <= self.limit, (name, self.off, self.limit)
        return t


def build_program(n_layers=2, stop_after=None, mixer_only=False, no_cc=False):
    nc = bass.Bass("TRN2", target_bir_lowering=False)
    S = Sched()

    def din(name, shape, dt=F32):
        return nc.dram_tensor(name, list(shape), dt, kind="ExternalInput").ap()

    x_d = din("x_fm", [NTF, 128, KC, TF])
    c_d = din("c_fm", [128, DM])
    wada_d = din("wada", [2, 72, 128, DM] if not mixer_only else [1, 1, 128, 16])
    bada_d = din("bada", [128, 144])
    normg_d = din("normg", [128, 48])
    gqk_d = din("gqk", [128, 4])
    convw_d = din("convw", [128, 24])
    convb_d = din("convb", [128, 8])
    flag_d = din("flag", [128, 1])
    mask_d = din("mask", [128, 512])
    w1_d = din("w1", [2, 2, NG1, 128, KC * 512] if not mixer_only else [1, 1, 1, 128, 16])
    w2_d = din("w2", [2, 2, 128, JC * 1024] if not mixer_only else [1, 1, 128, 16])
    win_d = din("win", [2, 128, KC * 3072])
    wout_d = din("wout", [2, 128, KC * 1024])
    y_d = nc.dram_tensor("y_fm", [NTF, 128, KC, TF], F32, kind="ExternalOutput").ap()
    xs_d = nc.dram_tensor("xs_scr", [NTF, 128, KC, TF], F32, kind="Internal").ap()
    CCW = (4 * HALO, 4 * HALO, 16)
    ccin_d = [[nc.dram_tensor(f"ccin{l}_{i}", [128, CCW[i]], BF16, kind="Internal").ap() for i in range(3)]
              for l in range(2)]
    ccout_d = [[nc.dram_tensor(f"ccout{l}_{i}", [256, CCW[i]], BF16, kind="Internal").ap() for i in range(3)]
               for l in range(2)]

    st = ExitStack()
    with st:
        SB0 = 16512
        SBTOP = 229344
        P = Arena(nc, SB0, SBTOP)
        onesmean = P.alloc([128, 128], BF16, "onesmean")
        blockmean = P.alloc([128, 128], BF16, "blockmean")
        identbf = P.alloc([128, 128], BF16, "ident")
        ones64 = P.alloc([128, 64], BF16, "ones64")
        flag64 = P.alloc([128, 64], BF16, "flag64")
        maskbf = P.alloc([128, 512], BF16, "maskbf")
        epsv = P.alloc([128, 1], F32, "epsv")
        mod = P.alloc([128, 144], F32, "mod")
        bada = P.alloc([128, 144], F32, "bada")
        normg = P.alloc([128, 48], F32, "normg")
        avec = P.alloc([128, 48], F32, "avec")
        gvec = P.alloc([128, 48], F32, "gvec")
        gqk = P.alloc([128, 4], F32, "gqk")
        gqs = P.alloc([128, 4], F32, "gqs")
        convw = P.alloc([128, 24], F32, "convw")
        convb = P.alloc([128, 8], F32, "convb")
        flag = P.alloc([128, 1], F32, "flag")
        modraw = P.alloc([128, 144], F32, "modraw")
        ptail = P.alloc([128, 4, 2], F32, "ptail")
        phalo = P.alloc([128, 4, 2], F32, "phalo")
        gb0 = P.alloc([128, 4, 2], F32, "gb0")
        ccdummy = P.alloc([128, 8], F32, "ccdummy")
        bccdummy = Buf("ccdummy")
        PH0 = (P.off + 63) // 64 * 64
        PHLIM = SBTOP

        ps = [st.enter_context(nc.psum_tensor(f"ps{i}", [128, 512], F32)) for i in range(7)]
        psT = st.enter_context(nc.psum_tensor("psT", [128, 1024], BF16))
        bps = [Buf(f"ps{i}") for i in range(7)]
        bpsT = Buf("psT")

        bconst = Buf("const")
        bvec = Buf("vec")

        A = Arena(nc, PH0, PHLIM)
        tmpf = A.alloc([128, 512], F32, "tmpf")
        tmpf2 = A.alloc([128, 512], F32, "tmpf2")
        btmp = Buf("tmpf")
        btmp2 = Buf("tmpf2")
        NWS = 8
        wslots = [A.alloc([128, DM], F32, "wadaslot") for _ in range(NWS)]
        bws = [Buf(f"ws{i}") for i in range(NWS)]
        cbc = A.alloc([128, DM], F32, "cbc")
        sbc = A.alloc([128, DM], F32, "sbc")
        mjunk = A.alloc([128, DM], BF16, "mjunk")
        bcbc, bsbc, bmjunk, bmodraw = Buf(), Buf(), Buf(), Buf()

        S.op("sp", lambda e: e.dma_start(out=cbc[:], in_=c_d), writes=[bcbc], dma=True)
        for (t, d) in ((bada, bada_d), (normg, normg_d), (gqk, gqk_d), (convw, convw_d),
                       (convb, convb_d), (flag, flag_d)):
            S.op("sp", lambda e, t=t, d=d: e.dma_start(out=t[:], in_=d), writes=[bvec], dma=True)
        S.op("sp", lambda e: e.dma_start(out=tmpf[:], in_=mask_d), writes=[btmp], dma=True)
        S.op("dve", lambda e: e.tensor_copy(out=maskbf[:], in_=tmpf[:]), reads=[btmp], writes=[bconst])
        S.op("pool", lambda e: e.memset(epsv[:], EPS), writes=[bconst])
        S.op("pool", lambda e: e.memset(onesmean[:], 1.0 / 1024.0), writes=[bconst])
        S.op("pool", lambda e: e.memset(ones64[:], 1.0), writes=[bconst])
        S.op("pool", lambda e: e.memset(blockmean[:], 0.0), writes=[bconst])
        S.op("pool", lambda e: e.memset(blockmean[0:64, 0:64], 1.0 / 64.0), writes=[bconst])
        S.op("pool", lambda e: e.memset(blockmean[64:128, 64:128], 1.0 / 64.0), writes=[bconst])
        S.op("pool", lambda e: e.memset(tmpf2[:, 0:128], 1.0), writes=[btmp2])
        S.op("pool", lambda e: e.affine_select(out=identbf[:], in_=tmpf2[:, 0:128], pattern=[[1, 128]],
                                               compare_op=ALU.is_equal, fill=0.0, base=0,
                                               channel_multiplier=-1),
             reads=[btmp2], writes=[bconst])
        S.op("dve", lambda e: e.tensor_scalar(out=flag64[:], in0=ones64[:], scalar1=flag[:, 0:1], scalar2=None,
                                              op0=ALU.mult), reads=[bconst, bvec], writes=[bconst])
        S.op("act", lambda e: e.activation(out=sbc[:], in_=cbc[:], func=AF.Silu), reads=[bcbc], writes=[bsbc])
        defer_mod1 = (n_layers > 1 and not mixer_only and (stop_after is None or stop_after >= 2))

        def derive(l):
            S.op("dve", lambda e, l=l: e.tensor_tensor(out=mod[:, l * 72:(l + 1) * 72], in0=modraw[:, l * 72:(l + 1) * 72],
                                                       in1=bada[:, l * 72:(l + 1) * 72], op=ALU.add),
                 reads=[bmodraw, bvec], writes=[bvec])
            for i in range(3):
                o8 = (l * 3 + i) * 8
                sc0 = l * 72 + (i * 3 + 1) * 8
                g0 = l * 72 + (i * 3 + 2) * 8
                S.op("dve", lambda e, o8=o8, sc0=sc0: e.scalar_tensor_tensor(
                    out=avec[:, o8:o8 + 8], in0=mod[:, sc0:sc0 + 8], scalar=1.0, in1=normg[:, o8:o8 + 8],
                    op0=ALU.add, op1=ALU.mult), reads=[bvec], writes=[bvec])
                S.op("dve", lambda e, o8=o8, g0=g0, i=i: e.tensor_scalar(
                    out=gvec[:, o8:o8 + 8], in0=mod[:, g0:g0 + 8], scalar1=(1.0 if i == 1 else 0.5),
                    scalar2=None, op0=ALU.mult), reads=[bvec], writes=[bvec])

        S.op("dve", lambda e: e.memset(modraw[:], 0.0), writes=[bmodraw])
        if not mixer_only:
            cnt = 0
            for l in range(1 if defer_mod1 else 2):
                for col in range(72):
                    s_ = cnt % NWS
                    cnt += 1
                    S.op("sp", lambda e, l=l, col=col, s_=s_: e.dma_start(out=wslots[s_][:], in_=wada_d[l, col]),
                         writes=[bws[s_]], dma=True)
                    S.op("dve", lambda e, l=l, col=col, s_=s_: e.scalar_tensor_tensor(
                        out=mjunk[:], in0=wslots[s_][:], scalar=1.0, in1=sbc[:], op0=ALU.mult, op1=ALU.mult,
                        accum_out=modraw[:, l * 72 + col:l * 72 + col + 1]),
                        reads=[bws[s_], bsbc], writes=[bmjunk, bmodraw])
        derive(0)
        if not defer_mod1:
            derive(1)
        S.op("dve", lambda e: e.tensor_scalar(out=gqs[:], in0=gqk[:], scalar1=0.125, scalar2=None, op0=ALU.mult),
             reads=[bvec], writes=[bvec])
        if mixer_only:
            S.op("dve", lambda e: e.memset(gvec[:], 1.0), writes=[bvec])

        def shift_ap(l, i, k):
            c0 = l * 72 + (i * 3 + 0) * 8 + k
            return mod[:, c0:c0 + 1]

        def a_ap(l, i, k):
            c0 = (l * 3 + i) * 8 + k
            return avec[:, c0:c0 + 1]

        def g_ap(l, i, k):
            c0 = (l * 3 + i) * 8 + k
            return gvec[:, c0:c0 + 1]

        def rms_affine(l, i, xt, bxt, T, xn, bxn, sq, bsq, mse, bmse, rstd, brstd, tt, btt, msbank, bmsbank, xg=()):
            for k in range(KC):
                s = k % 2
                S.op("act", lambda e, k=k, s=s: e.activation(out=sq[s][:, 0:T], in_=xt[:, k, :], func=AF.Square),
                     reads=[bxt] + list(xg), writes=[bsq[s]])
                S.op("pe", lambda e, k=k, s=s: e.matmul(msbank[:, 0:T], lhsT=onesmean[:], rhs=sq[s][:, 0:T],
                                                        start=(k == 0), stop=(k == KC - 1)),
                     reads=[bsq[s], bconst] + list(xg), writes=[bmsbank])
            S.op("act", lambda e: e.activation(out=mse[:, 0:T], in_=msbank[:, 0:T], func=AF.Ln, bias=epsv[:, 0:1],
                                               scale=1.0), reads=[bmsbank, bconst] + list(xg), writes=[bmse])
            S.op("act", lambda e: e.activation(out=rstd[:, 0:T], in_=mse[:, 0:T], func=AF.Exp, scale=-0.5),
                 reads=[bmse] + list(xg), writes=[brstd])
            for k in range(KC):
                s = k % 2
                S.op("dve", lambda e, k=k, s=s: e.scalar_tensor_tensor(
                    out=tt[s][:, 0:T], in0=xt[:, k, :], scalar=a_ap(l, i, k), in1=rstd[:, 0:T],
                    op0=ALU.mult, op1=ALU.mult), reads=[bxt, brstd, bvec] + list(xg), writes=[btt[s]])
                S.op("act", lambda e, k=k, s=s: e.activation(out=xn[:, k, :], in_=tt[s][:, 0:T], func=AF.Identity,
                                                             bias=shift_ap(l, i, k), scale=1.0),
                     reads=[btt[s], bvec] + list(xg), writes=[bxn[k]])

        def phase_ffn(l, f, src, dst):
            i = 0 if f == 0 else 2
            S.fence()
            A = Arena(nc, PH0, PHLIM)
            W1 = A.alloc([128, NG1, KC, 2, 256], BF16, "W1")
            W2 = A.alloc([128, JC, 1024], BF16, "W2")
            xs = A.alloc([128, KC, TF], F32, "xs")
            NR = 3
            xr = [A.alloc([128, TF], F32, "xr") for _ in range(NR)]
            xn2 = [A.alloc([128, KC, TF], BF16, "xn") for _ in range(2)]
            h = A.alloc([128, JC, TF], BF16, "h")
            sq = [A.alloc([128, TF], BF16, "sq") for _ in range(2)]
            sg = [A.alloc([128, TF], BF16, "sg") for _ in range(2)]
            mse = A.alloc([128, TF], F32, "mse")
            rstd = A.alloc([128, TF], F32, "rstd")
            tt1 = A.alloc([128, TF], F32, "tt")
            tt = [tt1, tt1]
            bW1 = [Buf() for _ in range(NG1)]
            W2P = ((0, 6), (6, 12), (12, 17), (17, 22))
            bW2 = [Buf() for _ in W2P]
            bxs = Buf()
            bxr = [Buf() for _ in range(NR)]
            bxn2 = [[Buf() for _ in range(KC)] for _ in range(2)]
            bh = [Buf() for _ in range(JC)]
            bsq = [Buf(), Buf()]
            bsg = [Buf(), Buf()]
            bmse, brstd = Buf(), Buf()
            btt1 = Buf()
            btt = [btt1, btt1]
            stores = []

            def load_x(t):
                S.op("sp", lambda e, t=t: e.dma_start(out=xs[:].rearrange("p k c -> p (k c)"),
                                                      in_=src[t].rearrange("p k c -> p (k c)")),
                     writes=[bxs], dma=True)

            load_x(0)
            for g in range(NG1):
                S.op("pool", lambda e, g=g: e.dma_start(out=W1[:, g].rearrange("p k u c -> p (k u c)"),
                                                        in_=w1_d[l, f, g]), writes=[bW1[g]], dma=True)
            for pi, (j0, j1) in enumerate(W2P):
                S.op("pool", lambda e, j0=j0, j1=j1: e.dma_start(
                    out=W2[:, j0:j1, :].rearrange("p j c -> p (j c)"),
                    in_=w2_d[l, f][:, j0 * 1024:j1 * 1024]), writes=[bW2[pi]], dma=True)

            def w2buf(j):
                for pi, (j0, j1) in enumerate(W2P):
                    if j0 <= j < j1:
                        return bW2[pi]

            def norm(t):
                rms_affine(l, i, xs, bxs, TF, xn2[t % 2], bxn2[t % 2], sq, bsq, mse, bmse, rstd, brstd, tt, btt,
                           ps[6], bps[6])
                if t + 1 < NTF:
                    load_x(t + 1)

            norm(0)
            for t in range(NTF):
                xn, bxn = xn2[t % 2], bxn2[t % 2]
                for j in range(JC):
                    gb, ub = ps[j % 2], ps[2 + j % 2]
                    bgb, bub = bps[j % 2], bps[2 + j % 2]
                    for u, (bank, bbank) in enumerate(((gb, bgb), (ub, bub))):
                        for k in range(KC):
                            S.op("pe", lambda e, j=j, k=k, u=u, bank=bank, xn=xn: e.matmul(
                                bank[:, :], lhsT=W1[:, j // 2, k, u, (j % 2) * 128:(j % 2) * 128 + 128],
                                rhs=xn[:, k, :], start=(k == 0), stop=(k == KC - 1)),
                                reads=[bW1[j // 2], bxn[k]], writes=[bbank])
                    s = j % 2
                    S.op("act", lambda e, s=s, gb=gb: e.activation(out=sg[s][:], in_=gb[:, :], func=AF.Silu),
                         reads=[bgb], writes=[bsg[s]])
                    S.op("dve", lambda e, s=s, ub=ub, j=j: e.tensor_tensor(out=h[:, j, :], in0=ub[:, :], in1=sg[s][:],
                                                                          op=ALU.mult),
                         reads=[bub, bsg[s]], writes=[bh[j]])
                if t + 1 < NTF:
                    norm(t + 1)

                def reload(t, d):
                    r = (t * KC + d) % NR
                    S.op("sp", lambda e, t=t, d=d, r=r: e.dma_start(out=xr[r][:], in_=src[t][:, d, :]),
                         writes=[bxr[r]], dma=True)

                for d in range(NR):
                    reload(t, d)
                for d in range(KC):
                    r = (t * KC + d) % NR
                    ob, bob = ps[4 + d % 2], bps[4 + d % 2]
                    for j in range(JC):
                        S.op("pe", lambda e, j=j, d=d, ob=ob: e.matmul(
                            ob[:, :], lhsT=W2[:, j, d * 128:(d + 1) * 128], rhs=h[:, j, :],
                            start=(j == 0), stop=(j == JC - 1)), reads=[w2buf(j), bh[j]], writes=[bob])
                    S.op("dve", lambda e, d=d, r=r, ob=ob: e.scalar_tensor_tensor(
                        out=xr[r][:], in0=ob[:, :], scalar=g_ap(l, i, d), in1=xr[r][:], op0=ALU.mult, op1=ALU.add),
                        reads=[bob, bxr[r], bvec], writes=[bxr[r]])
                    stores.append(S.op("sp", lambda e, t=t, d=d, r=r: e.dma_start(out=dst[t][:, d, :], in_=xr[r][:]),
                                       reads=[bxr[r]], dma=True))
                    if d + NR < KC:
                        reload(t, d + NR)
            return stores

        MIX = Arena(nc, PH0, PHLIM)
        Qs = MIX.alloc([128, 4, SC], BF16, "Q")
        Ks = MIX.alloc([128, 4, SC], BF16, "K")
        VTs = MIX.alloc([128, 4, SC], BF16, "VT")
        Woutc = MIX.alloc([128, 4, 1024], BF16, "Woutc")
        MIXEND = MIX.off

        def phase_mixer(l, src, mid, dst):
            S.fence()
            stores = []
            bQ = [[Buf() for _ in range(8)] for _ in range(4)]
            bK = [[Buf() for _ in range(8)] for _ in range(4)]
            bV = [[Buf() for _ in range(8)] for _ in range(4)]
            bWout = Buf()
            bptail, bphalo, bgb0 = Buf(), Buf(), Buf()

            A = Arena(nc, MIXEND, PHLIM)
            Win = A.alloc([128, KC, 3072], BF16, "Win")
            xs = A.alloc([128, KC, TB], F32, "xsB")
            NRB = 4
            xr = [A.alloc([128, TB], F32, "xrB") for _ in range(NRB)]
            h1 = A.alloc([128, KC, TB], BF16, "h1")
            nt_off = A.off
            sq = [A.alloc([128, TB], BF16, "sqB") for _ in range(2)]
            mse = [A.alloc([128, TB], F32, "mseB") for _ in range(2)]
            rstd = [A.alloc([128, TB], F32, "rstdB") for _ in range(2)]
            tt = [A.alloc([128, TB], F32, "ttB") for _ in range(2)]
            nt_end = A.off
            pcarry = A.alloc([128, 4, 2], F32, "pcarry")
            gdummy = A.alloc([128, 8], F32, "gdummy")
            AX = Arena(nc, nt_off, nt_end)
            G = Buf("alias_guard")
            bgd = Buf()
            gcs = AX.alloc([128, TB], F32, "gcs")
            pbuf = AX.alloc([128, TB + 2], F32, "pbuf")
            y1 = AX.alloc([128, TB], F32, "y1")
            y2 = AX.alloc([128, TB], F32, "y2")
            yc = AX.alloc([128, 4, TB], BF16, "yc")
            bWin = [Buf() for _ in range(KC)]
            bxs = Buf()
            bxr = [Buf() for _ in range(NRB)]
            bh1 = [Buf() for _ in range(KC)]
            bsq = [Buf(), Buf()]
            bmse = [Buf(), Buf()]
            brstd = [Buf(), Buf()]
            btt = [Buf(), Buf()]
            bgcs, bpbuf, by1, by2 = Buf(), Buf(), Buf(), Buf()
            byc = [Buf() for _ in range(4)]
            bpcarry = Buf()

            def load_x(t):
                S.op("sp", lambda e, t=t: e.dma_start(out=xs[:].rearrange("p k c -> p (k c)"),
                                                      in_=src[t].rearrange("p k c -> p (k c)")),
                     writes=[bxs], dma=True)

            def guard_switch():
                S.op("pool", lambda e: e.memset(gdummy[:], 0.0), writes=[G, bgd])

            load_x(0)
            for k in range(KC):
                S.op("pool", lambda e, k=k: e.dma_start(out=Win[:, k, :], in_=win_d[l][:, k * 3072:(k + 1) * 3072]),
                     writes=[bWin[k]], dma=True)
            S.op("pool", lambda e: e.dma_start(out=Woutc[:].rearrange("p k c -> p (k c)"),
                                               in_=wout_d[l][:, 4 * 1024:8 * 1024]),
                 writes=[bWout], dma=True)
            S.op("dve", lambda e: e.memset(pcarry[:], 0.0), writes=[bpcarry])

            pbank = [ps[0], ps[1], ps[2], ps[3]]
            bpbank = [bps[0], bps[1], bps[2], bps[3]]
            pbc = [0]

            def proj(col0, T):
                bi = pbc[0] % 4
                pbc[0] += 1
                bank, bb = pbank[bi], bpbank[bi]
                for k in range(KC):
                    S.op("pe", lambda e, k=k, bank=bank, col0=col0: e.matmul(
                        bank[:, 0:T], lhsT=Win[:, k, col0:col0 + 128], rhs=h1[:, k, :],
                        start=(k == 0), stop=(k == KC - 1)), reads=[bWin[k], bh1[k]], writes=[bb])
                return bank, bb

            xrc = 0
            nrm = 0
            for t in range(NTB):
                c0 = t * TB
                blk = c0 // 512
                rms_affine(l, 1, xs, bxs, TB, h1, bh1, sq, bsq, mse[0], bmse[0], rstd[0], brstd[0], tt, btt,
                           ps[6], bps[6], xg=[G])
                if t + 1 < NTB:
                    load_x(t + 1)
                qk_pend = []
                for qk in range(2):
                    for c in range(4):
                        bank, bb = proj(qk * 512 + c * 128, TB)
                        s = nrm % 2
                        nrm += 1
                        S.op("act", lambda e, s=s, bank=bank: e.activation(out=sq[s][:], in_=bank[:, 0:TB],
                                                                            func=AF.Square),
                             reads=[bb, G], writes=[bsq[s]])

                        def rest(qk=qk, c=c, s=s, bank=bank, bb=bb, c0=c0, blk=blk):
                            msb, bmsb = ps[4 + s], bps[4 + s]
                            S.op("pe", lambda e: e.matmul(msb[:, 0:TB], lhsT=blockmean[:], rhs=sq[s][:],
                                                          start=True, stop=True),
                                 reads=[bsq[s], bconst, G], writes=[bmsb])
                            S.op("act", lambda e: e.activation(
                                out=mse[s][:], in_=msb[:, 0:TB], func=AF.Ln, bias=epsv[:, 0:1], scale=1.0),
                                reads=[bmsb, bconst, G], writes=[bmse[s]])
                            S.op("act", lambda e: e.activation(out=rstd[s][:], in_=mse[s][:], func=AF.Exp, scale=-0.5),
                                 reads=[bmse[s], G], writes=[brstd[s]])
                            dstT = Qs if qk == 0 else Ks
                            bd = (bQ if qk == 0 else bK)[c][blk]
                            gsc = gqs[:, l * 2:l * 2 + 1] if qk == 0 else gqk[:, l * 2 + 1:l * 2 + 2]
                            S.op("dve", lambda e: e.scalar_tensor_tensor(
                                out=dstT[:, c, c0:c0 + TB], in0=bank[:, 0:TB], scalar=gsc, in1=rstd[s][:],
                                op0=ALU.mult, op1=ALU.mult), reads=[bb, brstd[s], bvec, G], writes=[bd])

                        if qk_pend:
                            qk_pend.pop(0)()
                        qk_pend.append(rest)
                for c in range(4):
                    bank, bb = proj(1024 + c * 128, TB)
                    if qk_pend:
                        qk_pend.pop(0)()
                    S.op("act", lambda e, bank=bank, c=c, c0=c0: e.activation(out=VTs[:, c, c0:c0 + TB],
                                                                            in_=bank[:, 0:TB], func=AF.Copy),
                         reads=[bb], writes=[bV[c][blk]])
                guard_switch()
                for c in range(4):
                    bank, bb = proj(2048 + c * 128, TB)
                    S.op("act", lambda e, bank=bank: e.activation(out=gcs[:], in_=bank[:, 0:TB], func=AF.Copy),
                         reads=[bb, G], writes=[bgcs])
                    bank, bb = proj(2560 + c * 128, TB)
                    S.op("pool", lambda e, c=c: e.tensor_copy(out=pbuf[:, 0:2], in_=pcarry[:, c, :]),
                         reads=[bpcarry, G], writes=[bpbuf])
                    S.op("dve", lambda e, bank=bank: e.tensor_tensor(out=pbuf[:, 2:TB + 2], in0=bank[:, 0:TB],
                                                                     in1=gcs[:], op=ALU.mult),
                         reads=[bb, bgcs, G], writes=[bpbuf])
                    S.op("pool", lambda e, c=c: e.tensor_copy(out=pcarry[:, c, :], in_=pbuf[:, TB:TB + 2]),
                         reads=[bpbuf, G], writes=[bpcarry])
                    if t == NTB - 1:
                        S.op("pool", lambda e, c=c: e.tensor_copy(out=ptail[:, c, :], in_=pbuf[:, TB:TB + 2]),
                             reads=[bpbuf, G], writes=[bptail])
                    w0 = convw[:, l * 12 + c * 3 + 0:l * 12 + c * 3 + 1]
                    w1 = convw[:, l * 12 + c * 3 + 1:l * 12 + c * 3 + 2]
                    w2 = convw[:, l * 12 + c * 3 + 2:l * 12 + c * 3 + 3]
                    cb = convb[:, l * 4 + c:l * 4 + c + 1]
                    S.op("act", lambda e, w2=w2, cb=cb: e.activation(out=y1[:], in_=pbuf[:, 2:TB + 2], func=AF.Identity,
                                                                     scale=w2, bias=cb),
                         reads=[bpbuf, bvec, G], writes=[by1])
                    S.op("dve", lambda e, w1=w1: e.scalar_tensor_tensor(out=y2[:], in0=pbuf[:, 1:TB + 1], scalar=w1,
                                                                        in1=y1[:], op0=ALU.mult, op1=ALU.add),
                         reads=[bpbuf, by1, bvec, G], writes=[by2])
                    S.op("dve", lambda e, w0=w0: e.scalar_tensor_tensor(out=y1[:], in0=pbuf[:, 0:TB], scalar=w0,
                                                                        in1=y2[:], op0=ALU.mult, op1=ALU.add),
                         reads=[bpbuf, by2, bvec, G], writes=[by1])
                    bank, bb = proj(1536 + c * 128, TB)
                    S.op("dve", lambda e, bank=bank, c=c: e.tensor_tensor(out=yc[:, c, :], in0=bank[:, 0:TB], in1=y1[:],
                                                                          op=ALU.mult),
                         reads=[bb, by1, G], writes=[byc[c]])
                    if t == 0:
                        S.op("act", lambda e, bank=bank, c=c: e.activation(out=gb0[:, c, :], in_=bank[:, 0:2],
                                                                          func=AF.Copy),
                             reads=[bb], writes=[bgb0])
                def reloadB(t, d):
                    r = (t * KC + d) % NRB
                    S.op("sp", lambda e, t=t, d=d, r=r: e.dma_start(out=xr[r][:], in_=src[t][:, d, :]),
                         writes=[bxr[r]], dma=True)

                for d in range(NRB):
                    reloadB(t, d)
                for d in range(KC):
                    r = (t * KC + d) % NRB
                    ob, bob = ps[4 + d % 2], bps[4 + d % 2]
                    for c in range(4):
                        S.op("pe", lambda e, c=c, d=d, ob=ob: e.matmul(
                            ob[:, 0:TB], lhsT=Woutc[:, c, d * 128:(d + 1) * 128], rhs=yc[:, c, :],
                            start=(c == 0), stop=(c == 3)), reads=[bWout, byc[c], G], writes=[bob])
                    S.op("dve", lambda e, d=d, r=r, ob=ob: e.scalar_tensor_tensor(
                        out=xr[r][:], in0=ob[:, 0:TB], scalar=g_ap(l, 1, d), in1=xr[r][:], op0=ALU.mult, op1=ALU.add),
                        reads=[bob, bxr[r], bvec], writes=[bxr[r]])
                    stores.append(S.op("sp", lambda e, t=t, d=d, r=r: e.dma_start(
                        out=mid[t][:, d, :], in_=xr[r][:]), reads=[bxr[r]], dma=True))
                    if d + NRB < KC:
                        reloadB(t, d + NRB)
                guard_switch()

            tailreads = [bK[c][b] for c in range(4) for b in range(4, 8)]
            w_k = S.op("sp", lambda e: e.dma_start(
                out=ccin_d[l][0][:, :].rearrange("p (c t) -> p c t", c=4), in_=Ks[:, :, SC - HALO:SC]),
                reads=tailreads, dma=True)
            tailreads = [bV[c][b] for c in range(4) for b in range(4, 8)]
            w_v = S.op("sp", lambda e: e.dma_start(
                out=ccin_d[l][1][:, :].rearrange("p (c t) -> p c t", c=4), in_=VTs[:, :, SC - HALO:SC]),
                reads=tailreads, dma=True)
            w_p = S.op("sp", lambda e: e.dma_start(
                out=ccin_d[l][2][:, :],
                in_=ptail[:].rearrange("p c t -> p (c t)").bitcast(BF16)), reads=[bptail], dma=True)
            bccout = []
            for i, wop in enumerate((w_k, w_v, w_p)):
                bi_ = Buf()
                bi_.writer = wop
                bo_ = Buf()
                bccout.append(bo_)
                if no_cc:
                    S.op("sp", lambda e, i=i: e.dma_start(out=ccout_d[l][i][0:128, :], in_=ccin_d[l][i]),
                         reads=[bi_], writes=[bo_], dma=True)
                else:
                    S.op("pool", lambda e, i=i: e.collective_compute(
                        "AllGather", ALU.bypass, replica_groups=[[0, 1], [2, 3], [4, 5], [6, 7]],
                        ins=[ccin_d[l][i]], outs=[ccout_d[l][i]]), reads=[bi_], writes=[bo_], own_sem=True)
            if not no_cc:
                S.op("pool", lambda e: e.memset(ccdummy[:], 0.0), reads=list(bccout), writes=[bccdummy])

            A = Arena(nc, MIXEND, PHLIM)
            Wouta = A.alloc([128, 4, 1024], BF16, "Wouta")
            ycorr = A.alloc([128, 4, 2], BF16, "ycorr")
            dy = A.alloc([128, 4, 2], F32, "dy")
            DOVL = A.off
            Kh = A.alloc([128, 4, HALO], BF16, "Kh")
            Vh = A.alloc([128, 4, HALO], BF16, "Vh")
            Nacc = A.alloc([128, HALO], F32, "Nacc")
            Dacc = A.alloc([128, HALO], F32, "Dacc")
            Vtok = [A.alloc([128, 32, 128], BF16, "Vtok") for _ in range(2)]
            PT = [A.alloc([128, 512], BF16, "PT") for _ in range(4)]
            rD = [A.alloc([128, 512], F32, "rD") for _ in range(2)]
            bWouta = Buf()
            do_mod = defer_mod1 and l == 0
            if do_mod:
                cs2 = A.alloc([128, DM], F32, "cs2")
                wsl2 = [A.alloc([128, DM], F32, "wsl2") for _ in range(2)]
                mj2 = A.alloc([128, DM], BF16, "mj2")
                bcs2, bmj2 = Buf(), Buf()
                bwsl2 = [Buf(), Buf()]
            bKh, bVh = Buf(), Buf()
            bNacc = [Buf() for _ in range(4)]
            bDacc = [Buf() for _ in range(4)]
            bVtok = [[Buf() for _ in range(8)] for _ in range(2)]
            bPT = [Buf() for _ in range(4)]
            brD = [Buf(), Buf()]
            S.fence()
            S.op("pool", lambda e: e.dma_start(out=Wouta[:].rearrange("p k c -> p (k c)"),
                                               in_=wout_d[l][:, 0:4 * 1024]), writes=[bWouta], dma=True)
            S.op("sp", lambda e: e.dma_start(
                out=Kh[:], in_=ccout_d[l][0][0:128, :].rearrange("p (c t) -> p c t", c=4)),
                reads=[bccout[0]], writes=[bKh], dma=True)
            S.op("sp", lambda e: e.dma_start(
                out=Vh[:], in_=ccout_d[l][1][0:128, :].rearrange("p (c t) -> p c t", c=4)),
                reads=[bccout[1]], writes=[bVh], dma=True)
            S.op("sp", lambda e: e.dma_start(
                out=phalo[:].rearrange("p c t -> p (c t)").bitcast(BF16),
                in_=ccout_d[l][2][0:128, :]), reads=[bccout[2]], writes=[bphalo], dma=True)

            modq = list(range(72)) if do_mod else []

            def mod_dma(col):
                S.op("sp", lambda e, col=col: e.dma_start(out=wsl2[col % 2][:], in_=wada_d[1, col]),
                     writes=[bwsl2[col % 2]], dma=True)

            if do_mod:
                S.op("sp", lambda e: e.dma_start(out=cs2[:], in_=c_d), writes=[bcs2], dma=True)
                S.op("act", lambda e: e.activation(out=cs2[:], in_=cs2[:], func=AF.Silu), reads=[bcs2], writes=[bcs2])
                mod_dma(0)
                mod_dma(1)

            def mod_step():
                if not modq:
                    return
                col = modq.pop(0)
                S.op("dve", lambda e, col=col: e.scalar_tensor_tensor(
                    out=mj2[:], in0=wsl2[col % 2][:], scalar=1.0, in1=cs2[:], op0=ALU.mult, op1=ALU.mult,
                    accum_out=modraw[:, 72 + col:72 + col + 1]),
                    reads=[bwsl2[col % 2], bcs2], writes=[bmj2, bmodraw])
                if col + 2 < 72:
                    mod_dma(col + 2)
                if not modq:
                    derive(1)

            def kcols(d, r, b):
                start = 128 * d * b + r
                if b < 0:
                    return True, HALO + start, HALO + start + 127 * d + 1
                return False, start, start + 127 * d + 1

            def blks_of(bl, hp, c0, c1):
                return [bl[hp][x] for x in range(c0 // 512, (c1 - 1) // 512 + 1)]

            sbc = [0]
            ptc = [0]
            vtc = [0]
            ndc = [0]
            LOOK = 2
            pend = []

            def flush(keep_pv=0):
                while sum(1 for k, _ in pend if k == "pv") > keep_pv:
                    pend.pop(0)[1]()
                if keep_pv == 0:
                    while pend:
                        pend.pop(0)[1]()

            def emit_pv(qpair, kbidx, vs, hh, half, pi, Nb, bNb, Db, bDb):
                for jq, (r, b) in enumerate(qpair):
                    ocol = (2 * half + jq) * 128
                    for pc, kb in enumerate(((r, b - 1), (r, b))):
                        slot = kbidx[kb]
                        col = (2 * jq + pc) * 128
                        onesT = flag64 if kb[1] < 0 else ones64
                        S.op("pe", lambda e, slot=slot, vs=vs, hh=hh, col=col, ocol=ocol, pi=pi, Nb=Nb, pc=pc: e.matmul(
                            Nb[64 * hh:64 * hh + 64, ocol:ocol + 128], lhsT=Vtok[vs][:, slot, 64 * hh:64 * hh + 64],
                            rhs=PT[pi][:, col:col + 128], start=(pc == 0), stop=(pc == 1)),
                            reads=[bVtok[vs][slot // 4], bPT[pi]], writes=[bNb])
                        S.op("pe", lambda e, hh=hh, col=col, ocol=ocol, pi=pi, Db=Db, pc=pc, onesT=onesT: e.matmul(
                            Db[64 * hh:64 * hh + 64, ocol:ocol + 128], lhsT=onesT[:, :],
                            rhs=PT[pi][:, col:col + 128], start=(pc == 0), stop=(pc == 1)),
                            reads=[bconst, bPT[pi]], writes=[bDb])

            def emit_acc(d, bi, g, Nb, bNb, Db, bDb):
                if d == 1:
                    accN = Nacc[:, 512 * g:512 * g + 512]
                    accD = Dacc[:, 512 * g:512 * g + 512]
                    srcN, srcD = Nb[:, :], Db[:, :]
                    blks = [g]
                elif d == 4:
                    accN = Nacc[:, 512 * g:512 * g + 512].rearrange("p (i r) -> p r i", r=4)
                    accD = Dacc[:, 512 * g:512 * g + 512].rearrange("p (i r) -> p r i", r=4)
                    srcN = Nb[:, :].rearrange("p (r i) -> p r i", r=4)
                    srcD = Db[:, :].rearrange("p (r i) -> p r i", r=4)
                    blks = [g]
                else:
                    accN = Nacc[:, :].rearrange("p (i r) -> p r i", r=16)[:, 4 * g:4 * g + 4, :]
                    accD = Dacc[:, :].rearrange("p (i r) -> p r i", r=16)[:, 4 * g:4 * g + 4, :]
                    srcN = Nb[:, :].rearrange("p (r i) -> p r i", r=4)
                    srcD = Db[:, :].rearrange("p (r i) -> p r i", r=4)
                    blks = [0, 1, 2, 3]
                if bi == 0:
                    S.op("act", lambda e: e.activation(out=accN, in_=srcN, func=AF.Copy),
                         reads=[bNb], writes=[bNacc[x] for x in blks])
                    S.op("dve", lambda e: e.tensor_copy(out=accD, in_=srcD),
                         reads=[bDb], writes=[bDacc[x] for x in blks])
                else:
                    S.op("dve", lambda e: e.tensor_tensor(out=accN, in0=srcN, in1=accN, op=ALU.add),
                         reads=[bNb] + [bNacc[x] for x in blks], writes=[bNacc[x] for x in blks])
                    S.op("dve", lambda e: e.tensor_tensor(out=accD, in0=srcD, in1=accD, op=ALU.add),
                         reads=[bDb] + [bDacc[x] for x in blks], writes=[bDacc[x] for x in blks])

            for w in (1, 0):
                for hp in range(4):
                    for bi, d in enumerate(BRANCH_D):
                        nb_w = 16 // d
                        kbs = []
                        for r in range(d):
                            for b in range(w * nb_w - 1, (w + 1) * nb_w):
                                kbs.append((r, b))
                        kbs = [kb for kb in kbs if kb[1] < 0] + [kb for kb in kbs if kb[1] >= 0]
                        nhalo = sum(1 for kb in kbs if kb[1] < 0)
                        kbidx = {kb: n for n, kb in enumerate(kbs)}
                        vs = vtc[0] % 2
                        vtc[0] += 1
                        for n0 in range(0, len(kbs), 8):
                            grp = kbs[n0:n0 + 8]
                            for n, (r, b) in enumerate(grp):
                                ish, c0, c1 = kcols(d, r, b)
                                srcT = Vh if ish else VTs
                                rb = [bVh] if ish else blks_of(bV, hp, c0, c1)
                                S.op("pe", lambda e, n=n, srcT=srcT, c0=c0, c1=c1, d=d, hp=hp: e.transpose(
                                    out=psT[:, n * 128:(n + 1) * 128], in_=srcT[:, hp, c0:c1:d], identity=identbf[:]),
                                    reads=rb + [bconst], writes=[bpsT])
                            nh = max(0, min(len(grp), nhalo - n0))
                            wb = [bVtok[vs][x] for x in range(n0 // 4, (n0 + len(grp) - 1) // 4 + 1)]
                            if nh > 0:
                                S.op("dve", lambda e, n0=n0, nh=nh, vs=vs: e.tensor_scalar(
                                    out=Vtok[vs][:, n0:n0 + nh, :].rearrange("p s c -> p (s c)"),
                                    in0=psT[:, 0:nh * 128], scalar1=flag[:, 0:1], scalar2=None, op0=ALU.mult),
                                    reads=[bpsT, bvec], writes=wb)
                            if nh < len(grp):
                                S.op("act", lambda e, n0=n0, nh=nh, vs=vs, ng=len(grp): e.activation(
                                    out=Vtok[vs][:, n0 + nh:n0 + ng, :].rearrange("p s c -> p (s c)"),
                                    in_=psT[:, nh * 128:ng * 128], func=AF.Copy),
                                    reads=[bpsT], writes=wb)
                        if d == 1:
                            groups = [[(0, w * 16 + 4 * g + j) for j in range(4)] for g in range(4)]
                        elif d == 4:
                            groups = [[(r, w * 4 + g) for r in range(4)] for g in range(4)]
                        else:
                            groups = [[(4 * g + r, w) for r in range(4)] for g in range(4)]
                        for g, qbs in enumerate(groups):
                            mod_step()
                            nslot = ndc[0] % 2
                            ndc[0] += 1
                            Nb, bNb = ps[3 + nslot * 2], bps[3 + nslot * 2]
                            Db, bDb = ps[4 + nslot * 2], bps[4 + nslot * 2]
                            for hh in range(2):
                                rows = slice(64 * hh, 64 * hh + 64)
                                for half in range(2):
                                    qpair = qbs[2 * half:2 * half + 2]
                                    si = sbc[0] % 3
                                    sbc[0] += 1
                                    Sb, bSb = ps[si], bps[si]
                                    pi = ptc[0] % 4
                                    ptc[0] += 1
                                    for jq, (r, b) in enumerate(qpair):
                                        _, q0, q1 = kcols(d, r, b)
                                        for pc, kb in enumerate(((r, b - 1), (r, b))):
                                            ish, c0, c1 = kcols(d, kb[0], kb[1])
                                            ksrc = Kh if ish else Ks
                                            rb = [bKh] if ish else blks_of(bK, hp, c0, c1)
                                            rb = rb + blks_of(bQ, hp, q0, q1)
                                            col = (2 * jq + pc) * 128
                                            kw = dict(tile_position=(64, 0)) if hh == 1 else {}
                                            S.op("pe", lambda e, ksrc=ksrc, c0=c0, c1=c1, q0=q0, q1=q1, d=d, hp=hp,
                                                 rows=rows, col=col, Sb=Sb, kw=kw: e.matmul(
                                                Sb[:, col:col + 128], lhsT=ksrc[rows, hp, c0:c1:d],
                                                rhs=Qs[rows, hp, q0:q1:d], start=True, stop=True, **kw),
                                                reads=rb, writes=[bSb])
                                    S.op("act", lambda e, Sb=Sb, pi=pi: e.activation(out=PT[pi][:], in_=Sb[:, :],
                                                                                     func=AF.Exp),
                                         reads=[bSb], writes=[bPT[pi]])
                                    S.op("dve", lambda e, pi=pi: e.tensor_tensor(out=PT[pi][:], in0=PT[pi][:],
                                                                                 in1=maskbf[:], op=ALU.mult),
                                         reads=[bPT[pi], bconst], writes=[bPT[pi]])
                                    flush(keep_pv=LOOK - 1)
                                    pend.append(("pv", lambda qpair=qpair, kbidx=kbidx, vs=vs, hh=hh, half=half, pi=pi,
                                                 Nb=Nb, bNb=bNb, Db=Db, bDb=bDb:
                                                 emit_pv(qpair, kbidx, vs, hh, half, pi, Nb, bNb, Db, bDb)))
                            pend.append(("acc", lambda d=d, bi=bi, g=g, Nb=Nb, bNb=bNb, Db=Db, bDb=bDb:
                                         emit_acc(d, bi, g, Nb, bNb, Db, bDb)))
                    flush(0)
                    for x in range(4):
                        s = x % 2
                        S.op("act", lambda e, x=x, s=s: e.activation(out=rD[s][:], in_=Dacc[:, 512 * x:512 * x + 512],
                                                                     func=AF.Ln), reads=[bDacc[x]], writes=[brD[s]])
                        S.op("act", lambda e, s=s: e.activation(out=rD[s][:], in_=rD[s][:], func=AF.Exp, scale=-1.0),
                             reads=[brD[s]], writes=[brD[s]])
                        qblk = (w * HALO + 512 * x) // 512
                        S.op("dve", lambda e, x=x, s=s, hp=hp, w=w: e.tensor_tensor(
                            out=Qs[:, hp, w * HALO + 512 * x:w * HALO + 512 * x + 512],
                            in0=Nacc[:, 512 * x:512 * x + 512], in1=rD[s][:], op=ALU.mult),
                            reads=[bNacc[x], brD[s]], writes=[bQ[hp][qblk]])

            S.fence()
            AD = Arena(nc, DOVL, PHLIM)
            NXT = 3
            xt = [AD.alloc([128, KC, TF], F32, "xtD") for _ in range(NXT)]
            bxt = [[Buf() for _ in range(KC)] for _ in range(NXT)]
            bycorr, bdy = Buf(), Buf()
            for c in range(4):
                w0 = convw[:, l * 12 + c * 3 + 0:l * 12 + c * 3 + 1]
                w1 = convw[:, l * 12 + c * 3 + 1:l * 12 + c * 3 + 2]
                S.op("dve", lambda e, c=c, w0=w0: e.tensor_scalar(out=dy[:, c, :], in0=phalo[:, c, :], scalar1=w0,
                                                                 scalar2=None, op0=ALU.mult),
                     reads=[bphalo, bvec], writes=[bdy])
                S.op("dve", lambda e, c=c, w1=w1: e.scalar_tensor_tensor(
                    out=dy[:, c, 0:1], in0=phalo[:, c, 1:2], scalar=w1, in1=dy[:, c, 0:1], op0=ALU.mult, op1=ALU.add),
                    reads=[bphalo, bdy, bvec], writes=[bdy])
            S.op("dve", lambda e: e.scalar_tensor_tensor(
                out=ycorr[:].rearrange("p c t -> p (c t)"), in0=dy[:].rearrange("p c t -> p (c t)"),
                scalar=flag[:, 0:1], in1=gb0[:].rearrange("p c t -> p (c t)"), op0=ALU.mult, op1=ALU.mult),
                reads=[bdy, bgb0, bvec], writes=[bycorr])

            def loadD(t):
                S.op("sp", lambda e, t=t: e.dma_start(out=xt[t % NXT][:].rearrange("p k c -> p (k c)"),
                                                      in_=mid[t].rearrange("p k c -> p (k c)")),
                     writes=bxt[t % NXT], dma=True)

            loadD(0)
            loadD(1)
            for t in range(NTF):
                sl = t % NXT
                for d in range(KC):
                    ob, bob = ps[d % 3], bps[d % 3]
                    for c in range(4):
                        S.op("pe", lambda e, c=c, d=d, ob=ob, t=t: e.matmul(
                            ob[:, :], lhsT=Wouta[:, c, d * 128:(d + 1) * 128], rhs=Qs[:, c, t * TF:(t + 1) * TF],
                            start=(c == 0), stop=(c == 3 and t != 0)), reads=[bWouta, bQ[c][t]], writes=[bob])
                    if t == 0:
                        for c in range(4):
                            S.op("pe", lambda e, c=c, d=d, ob=ob: e.matmul(
                                ob[:, 0:2], lhsT=Woutc[:, c, d * 128:(d + 1) * 128], rhs=ycorr[:, c, :],
                                start=False, stop=(c == 3)), reads=[bWout, bycorr], writes=[bob])
                    S.op("dve", lambda e, d=d, sl=sl, ob=ob: e.scalar_tensor_tensor(
                        out=xt[sl][:, d, :], in0=ob[:, :], scalar=g_ap(l, 1, d), in1=xt[sl][:, d, :],
                        op0=ALU.mult, op1=ALU.add), reads=[bob, bxt[sl][d], bvec], writes=[bxt[sl][d]])
                stores.append(S.op("sp", lambda e, t=t, sl=sl: e.dma_start(
                    out=dst[t].rearrange("p k c -> p (k c)"), in_=xt[sl][:].rearrange("p k c -> p (k c)")),
                    reads=bxt[sl], dma=True))
                if t + 2 < NTF:
                    loadD(t + 2)
            return stores

        plan = []
        for l in range(n_layers):
            plan += [("ffn", l, 0), ("mix", l, None), ("ffn", l, 1)]
        if stop_after is not None:
            plan = plan[:stop_after]
        if mixer_only:
            plan = [("mix", 0, None)]
        final = []
        for n, (kind, l, f) in enumerate(plan):
            src = x_d if n == 0 else xs_d
            dst = y_d if n == len(plan) - 1 else xs_d
            if kind == "ffn":
                final = phase_ffn(l, f, src, dst)
            else:
                final = phase_mixer(l, src, xs_d if mixer_only else src, dst)
        S.run_block(nc, st, final_waits=final)
    return nc


_CACHE = {}


def _host_layouts(x, c, w_ada, b_ada, norm_g, w_in, q_norm_g, k_norm_g, conv_w, conv_b, w_out, ffn_w1, ffn_w2):
    f = np.float32
    asc = np.ascontiguousarray
    shared = {}
    shared["wada"] = asc(w_ada.reshape(2, DM, 72, 128).transpose(0, 2, 3, 1))
    shared["bada"] = asc(b_ada.reshape(2, 72, 128).transpose(2, 0, 1).reshape(128, 144))
    shared["normg"] = asc(norm_g.reshape(2, 3, KC, 128).transpose(3, 0, 1, 2).reshape(128, 48))
    gq = np.concatenate([q_norm_g, q_norm_g], axis=1)
    gk = np.concatenate([k_norm_g, k_norm_g], axis=1)
    shared["gqk"] = asc(np.stack([gq[0], gk[0], gq[1], gk[1]], axis=1))
    shared["convw"] = asc(conv_w.reshape(2, 3, 4, 128).transpose(3, 0, 2, 1).reshape(128, 24))
    shared["convb"] = asc(conv_b.reshape(2, 4, 128).transpose(2, 0, 1).reshape(128, 8))
    k = np.arange(128)[:, None]
    q = np.arange(128)[None, :]
    prev = (k >= q).astype(f)
    cur = (k <= q).astype(f)
    shared["mask"] = asc(np.concatenate([prev, cur, prev, cur], axis=1))
    w1 = ffn_w1.reshape(2, 2, KC, 128, 2, NG1, 256)
    shared["w1"] = asc(w1.transpose(0, 1, 5, 3, 2, 4, 6)).reshape(2, 2, NG1, 128, KC * 512)
    shared["w2"] = asc(ffn_w2.reshape(2, 2, JC, 128, 1024).transpose(0, 1, 3, 2, 4)).reshape(2, 2, 128, JC * 1024)
    shared["win"] = asc(w_in.reshape(2, KC, 128, 3072).transpose(0, 2, 1, 3)).reshape(2, 128, KC * 3072)
    shared["wout"] = asc(w_out.reshape(2, KC, 128, 1024).transpose(0, 2, 1, 3)).reshape(2, 128, KC * 1024)
    in_maps = []
    for core in range(NCORES):
        b, half = core // 2, core % 2
        xc = x[b, half * SC:(half + 1) * SC]
        xfm = asc(xc.reshape(NTF, TF, KC, 128).transpose(0, 3, 2, 1))
        m = dict(shared)
        m["x_fm"] = xfm
        m["c_fm"] = asc(np.broadcast_to(c[b][None, :], (128, DM)))
        m["flag"] = np.full((128, 1), float(half), dtype=f)
        in_maps.append(m)
    return in_maps


def kernel(x, c, w_ada, b_ada, norm_g, w_in, q_norm_g, k_norm_g, conv_w, conv_b, w_out, ffn_w1, ffn_w2,
           _stop_after=None, _mixer_only=False, _no_cc=False):
    args = [np.asarray(a, dtype=np.float32) for a in
            (x, c, w_ada, b_ada, norm_g, w_in, q_norm_g, k_norm_g, conv_w, conv_b, w_out, ffn_w1, ffn_w2)]
    in_maps = _host_layouts(*args)
    key = ("nc", _stop_after, _mixer_only, _no_cc)
    if key not in _CACHE:
        _CACHE[key] = build_program(stop_after=_stop_after, mixer_only=_mixer_only, no_cc=_no_cc)
    if _mixer_only:
        for m in in_maps:
            m["wada"] = np.zeros((1, 1, 128, 16), np.float32)
            m["w1"] = np.zeros((1, 1, 1, 128, 16), np.float32)
            m["w2"] = np.zeros((1, 1, 128, 16), np.float32)
    nc = _CACHE[key]
    res = run_bass_kernel_spmd(nc, in_maps, core_ids=list(range(NCORES)))
    out = np.empty((4, 8192, DM), dtype=np.float32)
    for core in range(NCORES):
        b, half = core // 2, core % 2
        yfm = np.asarray(res.results[core]["y_fm"])
        out[b, half * SC:(half + 1) * SC] = yfm.transpose(0, 3, 2, 1).reshape(SC, DM)
    return out
```

```python
import numpy as np
from contextlib import ExitStack
import concourse.bass as bass
import concourse.mybir as mybir
from concourse.bass_utils import run_bass_kernel_spmd

F32 = mybir.dt.float32
BF16 = mybir.dt.bfloat16
AF = mybir.ActivationFunctionType
ALU = mybir.AluOpType

ENGS = ("pe", "act", "dve", "pool", "sp")
GEN = 20000
NDMASEM = 12

NCORES = 8
DM = 1024
KC = 8
SC = 4096
TF = 512
NTF = SC // TF
TB = 512
NTB = SC // TB
DFF = 2816
JC = 22
NG1 = 11
HALO = 2048
EPS = 1e-6
FCC = 4 * HALO + 4 * HALO + 16
BRANCH_D = (1, 4, 16)


class Buf:
    __slots__ = ("name", "writer", "readers")

    def __init__(self, name=""):
        self.name = name
        self.writer = None
        self.readers = []


class Op:
    __slots__ = ("eng", "fn", "deps", "dma", "idx", "need_sig", "sigval", "sem", "dmaslot",
                 "dmaprev", "own")

    def __init__(self, eng, fn, dma):
        self.own = False
        self.eng = eng
        self.fn = fn
        self.dma = dma
        self.deps = []
        self.need_sig = False
        self.sigval = None
        self.sem = None
        self.dmaslot = None
        self.dmaprev = None


class Sched:
    def __init__(self):
        self.q = {e: [] for e in ENGS}
        self.ndma = {e: 0 for e in ENGS}
        self.fence_deps = {e: [] for e in ENGS}

    def fence(self):
        deps = []
        for e in ENGS:
            q = self.q[e]
            last_c = None
            nd = 0
            for o in reversed(q):
                if o.own:
                    continue
                if o.dma:
                    if nd < NDMASEM + 2:
                        deps.append(o)
                        nd += 1
                elif last_c is None:
                    last_c = o
                    deps.append(o)
                if last_c is not None and nd >= NDMASEM + 2:
                    break
        for e in ENGS:
            self.fence_deps[e] = list(deps)

    def op(self, eng, fn, reads=(), writes=(), dma=False, own_sem=False):
        o = Op(eng, fn, dma)
        o.own = own_sem
        o.idx = len(self.q[eng])
        deps = []
        for b in reads:
            if b.writer is not None:
                deps.append(b.writer)
        for b in writes:
            if b.writer is not None:
                deps.append(b.writer)
            deps.extend(b.readers)
        seen = set()
        for d in deps:
            if d is o or id(d) in seen:
                continue
            seen.add(id(d))
            if d.eng == eng and not d.dma and not dma and not d.own:
                if eng == "pe":
                    continue
                if eng != "pool" and not any(b.writer is d for b in reads):
                    continue
            o.deps.append(d)
            d.need_sig = True
        if self.fence_deps[eng]:
            for d in self.fence_deps[eng]:
                if id(d) in seen:
                    continue
                seen.add(id(d))
                if d.eng == eng and not d.dma and not d.own:
                    continue
                o.deps.append(d)
                d.need_sig = True
            self.fence_deps[eng] = []
        for b in reads:
            b.readers.append(o)
        for b in writes:
            b.writer = o
            b.readers = []
        if dma:
            o.dmaslot = self.ndma[eng] % NDMASEM
            self.ndma[eng] += 1
        self.q[eng].append(o)
        return o

    def finalize(self, nc, stack):
        self.sems = {}

        def getsem(name):
            if name not in self.sems:
                self.sems[name] = stack.enter_context(nc.semaphore(name))
            return self.sems[name]

        for e in ENGS:
            cnt = 0
            gen = 0
            dcnt = [0] * NDMASEM
            dlast = [None] * NDMASEM
            for o in self.q[e]:
                if o.own:
                    o.sem = getsem(f"own_{e}_{o.idx}")
                    o.sigval = 1
                elif o.dma:
                    s = o.dmaslot
                    o.dmaprev = dlast[s]
                    dcnt[s] += 16
                    o.sem = getsem(f"d_{e}_{s}")
                    o.sigval = dcnt[s]
                    dlast[s] = o
                elif o.need_sig:
                    if cnt >= GEN:
                        gen += 1
                        cnt = 0
                    cnt += 1
                    o.sem = getsem(f"c_{e}_{gen}")
                    o.sigval = cnt

    def emit(self, eng, handle):
        waited = {}
        for o in self.q[eng]:
            deps = list(o.deps)
            if o.dma and o.dmaprev is not None:
                deps.append(o.dmaprev)
            for d in deps:
                key = id(d.sem)
                if waited.get(key, 0) >= d.sigval:
                    continue
                handle.wait_ge(d.sem, d.sigval)
                waited[key] = d.sigval
            ins = o.fn(handle)
            if o.own:
                ins.then_inc(o.sem, 1)
            elif o.dma:
                ins.then_inc(o.sem, 16)
            elif o.need_sig:
                ins.then_inc(o.sem, 1)

    def run_block(self, nc, stack, final_waits=()):
        self.finalize(nc, stack)
        S = self
        with nc.Block() as block:
            @block.tensor
            def _(h):
                S.emit("pe", h)

            @block.scalar
            def _(h):
                S.emit("act", h)

            @block.vector
            def _(h):
                S.emit("dve", h)

            @block.gpsimd
            def _(h):
                S.emit("pool", h)

            @block.sync
            def _(h):
                S.emit("sp", h)
                for o in final_waits:
                    h.wait_ge(o.sem, o.sigval)


class Arena:
    cnt = [0]

    def __init__(self, nc, start, limit):
        self.nc = nc
        self.off = start
        self.limit = limit

    def alloc(self, shape, dt, name="t"):
        n = int(np.prod(shape[1:])) * (4 if dt == F32 else 2)
        Arena.cnt[0] += 1
        t = self.nc.alloc_sbuf_tensor_at(f"{name}{Arena.cnt[0]}", list(shape), dt, offset=self.off)
        self.off += (n + 31) // 32 * 32
        assert self.off <= self.limit, (name, self.off, self.limit)
        return t


def build_program(n_layers=2, stop_after=None, mixer_only=False, no_cc=False):
    nc = bass.Bass("TRN2", target_bir_lowering=False)
    S = Sched()

    def din(name, shape, dt=F32):
        return nc.dram_tensor(name, list(shape), dt, kind="ExternalInput").ap()

    x_d = din("x_fm", [NTF, 128, KC, TF])
    c_d = din("c_fm", [128, DM])
    wada_d = din("wada", [2, 72, 128, DM] if not mixer_only else [1, 1, 128, 16])
    bada_d = din("bada", [128, 144])
    normg_d = din("normg", [128, 48])
    gqk_d = din("gqk", [128, 4])
    convw_d = din("convw", [128, 24])
    convb_d = din("convb", [128, 8])
    flag_d = din("flag", [128, 1])
    mask_d = din("mask", [128, 512])
    w1_d = din("w1", [2, 2, NG1, 128, KC * 512] if not mixer_only else [1, 1, 1, 128, 16])
    w2_d = din("w2", [2, 2, 128, JC * 1024] if not mixer_only else [1, 1, 128, 16])
    win_d = din("win", [2, 128, KC * 3072])
    wout_d = din("wout", [2, 128, KC * 1024])
    y_d = nc.dram_tensor("y_fm", [NTF, 128, KC, TF], F32, kind="ExternalOutput").ap()
    xs_d = nc.dram_tensor("xs_scr", [NTF, 128, KC, TF], F32, kind="Internal").ap()
    CCW = (4 * HALO, 4 * HALO, 16)
    ccin_d = [[nc.dram_tensor(f"ccin{l}_{i}", [128, CCW[i]], BF16, kind="Internal").ap() for i in range(3)]
              for l in range(2)]
    ccout_d = [[nc.dram_tensor(f"ccout{l}_{i}", [256, CCW[i]], BF16, kind="Internal").ap() for i in range(3)]
               for l in range(2)]

    st = ExitStack()
    with st:
        SB0 = 16512
        SBTOP = 229344
        P = Arena(nc, SB0, SBTOP)
        onesmean = P.alloc([128, 128], BF16, "onesmean")
        blockmean = P.alloc([128, 128], BF16, "blockmean")
        identbf = P.alloc([128, 128], BF16, "ident")
        ones64 = P.alloc([128, 64], BF16, "ones64")
        flag64 = P.alloc([128, 64], BF16, "flag64")
        maskbf = P.alloc([128, 512], BF16, "maskbf")
        epsv = P.alloc([128, 1], F32, "epsv")
        mod = P.alloc([128, 144], F32, "mod")
        bada = P.alloc([128, 144], F32, "bada")
        normg = P.alloc([128, 48], F32, "normg")
        avec = P.alloc([128, 48], F32, "avec")
        gvec = P.alloc([128, 48], F32, "gvec")
        gqk = P.alloc([128, 4], F32, "gqk")
        gqs = P.alloc([128, 4], F32, "gqs")
        convw = P.alloc([128, 24], F32, "convw")
        convb = P.alloc([128, 8], F32, "convb")
        flag = P.alloc([128, 1], F32, "flag")
        modraw = P.alloc([128, 144], F32, "modraw")
        ptail = P.alloc([128, 4, 2], F32, "ptail")
        phalo = P.alloc([128, 4, 2], F32, "phalo")
        gb0 = P.alloc([128, 4, 2], F32, "gb0")
        ccdummy = P.alloc([128, 8], F32, "ccdummy")
        bccdummy = Buf("ccdummy")
        PH0 = (P.off + 63) // 64 * 64
        PHLIM = SBTOP

        ps = [st.enter_context(nc.psum_tensor(f"ps{i}", [128, 512], F32)) for i in range(7)]
        psT = st.enter_context(nc.psum_tensor("psT", [128, 1024], BF16))
        bps = [Buf(f"ps{i}") for i in range(7)]
        bpsT = Buf("psT")

        bconst = Buf("const")
        bvec = Buf("vec")

        A = Arena(nc, PH0, PHLIM)
        tmpf = A.alloc([128, 512], F32, "tmpf")
        tmpf2 = A.alloc([128, 512], F32, "tmpf2")
        btmp = Buf("tmpf")
        btmp2 = Buf("tmpf2")
        NWS = 8
        wslots = [A.alloc([128, DM], F32, "wadaslot") for _ in range(NWS)]
        bws = [Buf(f"ws{i}") for i in range(NWS)]
        cbc = A.alloc([128, DM], F32, "cbc")
        sbc = A.alloc([128, DM], F32, "sbc")
        mjunk = A.alloc([128, DM], BF16, "mjunk")
        bcbc, bsbc, bmjunk, bmodraw = Buf(), Buf(), Buf(), Buf()

        S.op("sp", lambda e: e.dma_start(out=cbc[:], in_=c_d), writes=[bcbc], dma=True)
        for (t, d) in ((bada, bada_d), (normg, normg_d), (gqk, gqk_d), (convw, convw_d),
                       (convb, convb_d), (flag, flag_d)):
            S.op("sp", lambda e, t=t, d=d: e.dma_start(out=t[:], in_=d), writes=[bvec], dma=True)
        S.op("sp", lambda e: e.dma_start(out=tmpf[:], in_=mask_d), writes=[btmp], dma=True)
        S.op("dve", lambda e: e.tensor_copy(out=maskbf[:], in_=tmpf[:]), reads=[btmp], writes=[bconst])
        S.op("pool", lambda e: e.memset(epsv[:], EPS), writes=[bconst])
        S.op("pool", lambda e: e.memset(onesmean[:], 1.0 / 1024.0), writes=[bconst])
        S.op("pool", lambda e: e.memset(ones64[:], 1.0), writes=[bconst])
        S.op("pool", lambda e: e.memset(blockmean[:], 0.0), writes=[bconst])
        S.op("pool", lambda e: e.memset(blockmean[0:64, 0:64], 1.0 / 64.0), writes=[bconst])
        S.op("pool", lambda e: e.memset(blockmean[64:128, 64:128], 1.0 / 64.0), writes=[bconst])
        S.op("pool", lambda e: e.memset(tmpf2[:, 0:128], 1.0), writes=[btmp2])
        S.op("pool", lambda e: e.affine_select(out=identbf[:], in_=tmpf2[:, 0:128], pattern=[[1, 128]],
                                               compare_op=ALU.is_equal, fill=0.0, base=0,
                                               channel_multiplier=-1),
             reads=[btmp2], writes=[bconst])
        S.op("dve", lambda e: e.tensor_scalar(out=flag64[:], in0=ones64[:], scalar1=flag[:, 0:1], scalar2=None,
                                              op0=ALU.mult), reads=[bconst, bvec], writes=[bconst])
        S.op("act", lambda e: e.activation(out=sbc[:], in_=cbc[:], func=AF.Silu), reads=[bcbc], writes=[bsbc])
        defer_mod1 = (n_layers > 1 and not mixer_only and (stop_after is None or stop_after >= 2))

        def derive(l):
            S.op("dve", lambda e, l=l: e.tensor_tensor(out=mod[:, l * 72:(l + 1) * 72], in0=modraw[:, l * 72:(l + 1) * 72],
                                                       in1=bada[:, l * 72:(l + 1) * 72], op=ALU.add),
                 reads=[bmodraw, bvec], writes=[bvec])
            for i in range(3):
                o8 = (l * 3 + i) * 8
                sc0 = l * 72 + (i * 3 + 1) * 8
                g0 = l * 72 + (i * 3 + 2) * 8
                S.op("dve", lambda e, o8=o8, sc0=sc0: e.scalar_tensor_tensor(
                    out=avec[:, o8:o8 + 8], in0=mod[:, sc0:sc0 + 8], scalar=1.0, in1=normg[:, o8:o8 + 8],
                    op0=ALU.add, op1=ALU.mult), reads=[bvec], writes=[bvec])
                S.op("dve", lambda e, o8=o8, g0=g0, i=i: e.tensor_scalar(
                    out=gvec[:, o8:o8 + 8], in0=mod[:, g0:g0 + 8], scalar1=(1.0 if i == 1 else 0.5),
                    scalar2=None, op0=ALU.mult), reads=[bvec], writes=[bvec])

        S.op("dve", lambda e: e.memset(modraw[:], 0.0), writes=[bmodraw])
        if not mixer_only:
            cnt = 0
            for l in range(1 if defer_mod1 else 2):
                for col in range(72):
                    s_ = cnt % NWS
                    cnt += 1
                    S.op("sp", lambda e, l=l, col=col, s_=s_: e.dma_start(out=wslots[s_][:], in_=wada_d[l, col]),
                         writes=[bws[s_]], dma=True)
                    S.op("dve", lambda e, l=l, col=col, s_=s_: e.scalar_tensor_tensor(
                        out=mjunk[:], in0=wslots[s_][:], scalar=1.0, in1=sbc[:], op0=ALU.mult, op1=ALU.mult,
                        accum_out=modraw[:, l * 72 + col:l * 72 + col + 1]),
                        reads=[bws[s_], bsbc], writes=[bmjunk, bmodraw])
        derive(0)
        if not defer_mod1:
            derive(1)
        S.op("dve", lambda e: e.tensor_scalar(out=gqs[:], in0=gqk[:], scalar1=0.125, scalar2=None, op0=ALU.mult),
             reads=[bvec], writes=[bvec])
        if mixer_only:
            S.op("dve", lambda e: e.memset(gvec[:], 1.0), writes=[bvec])

        def shift_ap(l, i, k):
            c0 = l * 72 + (i * 3 + 0) * 8 + k
            return mod[:, c0:c0 + 1]

        def a_ap(l, i, k):
            c0 = (l * 3 + i) * 8 + k
            return avec[:, c0:c0 + 1]

        def g_ap(l, i, k):
            c0 = (l * 3 + i) * 8 + k
            return gvec[:, c0:c0 + 1]

        def rms_affine(l, i, xt, bxt, T, xn, bxn, sq, bsq, mse, bmse, rstd, brstd, tt, btt, msbank, bmsbank, xg=()):
            for k in range(KC):
                s = k % 2
                S.op("act", lambda e, k=k, s=s: e.activation(out=sq[s][:, 0:T], in_=xt[:, k, :], func=AF.Square),
                     reads=[bxt] + list(xg), writes=[bsq[s]])
                S.op("pe", lambda e, k=k, s=s: e.matmul(msbank[:, 0:T], lhsT=onesmean[:], rhs=sq[s][:, 0:T],
                                                        start=(k == 0), stop=(k == KC - 1)),
                     reads=[bsq[s], bconst] + list(xg), writes=[bmsbank])
            S.op("act", lambda e: e.activation(out=mse[:, 0:T], in_=msbank[:, 0:T], func=AF.Ln, bias=epsv[:, 0:1],
                                               scale=1.0), reads=[bmsbank, bconst] + list(xg), writes=[bmse])
            S.op("act", lambda e: e.activation(out=rstd[:, 0:T], in_=mse[:, 0:T], func=AF.Exp, scale=-0.5),
                 reads=[bmse] + list(xg), writes=[brstd])
            for k in range(KC):
                s = k % 2
                S.op("dve", lambda e, k=k, s=s: e.scalar_tensor_tensor(
                    out=tt[s][:, 0:T], in0=xt[:, k, :], scalar=a_ap(l, i, k), in1=rstd[:, 0:T],
                    op0=ALU.mult, op1=ALU.mult), reads=[bxt, brstd, bvec] + list(xg), writes=[btt[s]])
                S.op("act", lambda e, k=k, s=s: e.activation(out=xn[:, k, :], in_=tt[s][:, 0:T], func=AF.Identity,
                                                             bias=shift_ap(l, i, k), scale=1.0),
                     reads=[btt[s], bvec] + list(xg), writes=[bxn[k]])

        def phase_ffn(l, f, src, dst):
            i = 0 if f == 0 else 2
            S.fence()
            A = Arena(nc, PH0, PHLIM)
            W1 = A.alloc([128, NG1, KC, 2, 256], BF16, "W1")
            W2 = A.alloc([128, JC, 1024], BF16, "W2")
            xs = A.alloc([128, KC, TF], F32, "xs")
            NR = 3
            xr = [A.alloc([128, TF], F32, "xr") for _ in range(NR)]
            xn2 = [A.alloc([128, KC, TF], BF16, "xn") for _ in range(2)]
            h = A.alloc([128, JC, TF], BF16, "h")
            sq = [A.alloc([128, TF], BF16, "sq") for _ in range(2)]
            sg = [A.alloc([128, TF], BF16, "sg") for _ in range(2)]
            mse = A.alloc([128, TF], F32, "mse")
            rstd = A.alloc([128, TF], F32, "rstd")
            tt1 = A.alloc([128, TF], F32, "tt")
            tt = [tt1, tt1]
            bW1 = [Buf() for _ in range(NG1)]
            W2P = ((0, 6), (6, 12), (12, 17), (17, 22))
            bW2 = [Buf() for _ in W2P]
            bxs = Buf()
            bxr = [Buf() for _ in range(NR)]
            bxn2 = [[Buf() for _ in range(KC)] for _ in range(2)]
            bh = [Buf() for _ in range(JC)]
            bsq = [Buf(), Buf()]
            bsg = [Buf(), Buf()]
            bmse, brstd = Buf(), Buf()
            btt1 = Buf()
            btt = [btt1, btt1]
            stores = []

            def load_x(t):
                S.op("sp", lambda e, t=t: e.dma_start(out=xs[:].rearrange("p k c -> p (k c)"),
                                                      in_=src[t].rearrange("p k c -> p (k c)")),
                     writes=[bxs], dma=True)

            load_x(0)
            for g in range(NG1):
                S.op("pool", lambda e, g=g: e.dma_start(out=W1[:, g].rearrange("p k u c -> p (k u c)"),
                                                        in_=w1_d[l, f, g]), writes=[bW1[g]], dma=True)
            for pi, (j0, j1) in enumerate(W2P):
                S.op("pool", lambda e, j0=j0, j1=j1: e.dma_start(
                    out=W2[:, j0:j1, :].rearrange("p j c -> p (j c)"),
                    in_=w2_d[l, f][:, j0 * 1024:j1 * 1024]), writes=[bW2[pi]], dma=True)

            def w2buf(j):
                for pi, (j0, j1) in enumerate(W2P):
                    if j0 <= j < j1:
                        return bW2[pi]

            def norm(t):
                rms_affine(l, i, xs, bxs, TF, xn2[t % 2], bxn2[t % 2], sq, bsq, mse, bmse, rstd, brstd, tt, btt,
                           ps[6], bps[6])
                if t + 1 < NTF:
                    load_x(t + 1)

            norm(0)
            for t in range(NTF):
                xn, bxn = xn2[t % 2], bxn2[t % 2]
                for j in range(JC):
                    gb, ub = ps[j % 2], ps[2 + j % 2]
                    bgb, bub = bps[j % 2], bps[2 + j % 2]
                    for u, (bank, bbank) in enumerate(((gb, bgb), (ub, bub))):
                        for k in range(KC):
                            S.op("pe", lambda e, j=j, k=k, u=u, bank=bank, xn=xn: e.matmul(
                                bank[:, :], lhsT=W1[:, j // 2, k, u, (j % 2) * 128:(j % 2) * 128 + 128],
                                rhs=xn[:, k, :], start=(k == 0), stop=(k == KC - 1)),
                                reads=[bW1[j // 2], bxn[k]], writes=[bbank])
                    s = j % 2
                    S.op("act", lambda e, s=s, gb=gb: e.activation(out=sg[s][:], in_=gb[:, :], func=AF.Silu),
                         reads=[bgb], writes=[bsg[s]])
                    S.op("dve", lambda e, s=s, ub=ub, j=j: e.tensor_tensor(out=h[:, j, :], in0=ub[:, :], in1=sg[s][:],
                                                                          op=ALU.mult),
                         reads=[bub, bsg[s]], writes=[bh[j]])
                def reload(t, d):
                    r = (t * KC + d) % NR
                    S.op("sp", lambda e, t=t, d=d, r=r: e.dma_start(out=xr[r][:], in_=src[t][:, d, :]),
                         writes=[bxr[r]], dma=True)

                for d in range(NR):
                    reload(t, d)
                if t + 1 < NTF:
                    norm(t + 1)
                for d in range(KC):
                    r = (t * KC + d) % NR
                    ob, bob = ps[4 + d % 2], bps[4 + d % 2]
                    for j in range(JC):
                        S.op("pe", lambda e, j=j, d=d, ob=ob: e.matmul(
                            ob[:, :], lhsT=W2[:, j, d * 128:(d + 1) * 128], rhs=h[:, j, :],
                            start=(j == 0), stop=(j == JC - 1)), reads=[w2buf(j), bh[j]], writes=[bob])
                    S.op("dve", lambda e, d=d, r=r, ob=ob: e.scalar_tensor_tensor(
                        out=xr[r][:], in0=ob[:, :], scalar=g_ap(l, i, d), in1=xr[r][:], op0=ALU.mult, op1=ALU.add),
                        reads=[bob, bxr[r], bvec], writes=[bxr[r]])
                    stores.append(S.op("sp", lambda e, t=t, d=d, r=r: e.dma_start(out=dst[t][:, d, :], in_=xr[r][:]),
                                       reads=[bxr[r]], dma=True))
                    if d + NR < KC:
                        reload(t, d + NR)
            return stores

        MIX = Arena(nc, PH0, PHLIM)
        Qs = MIX.alloc([128, 4, SC], BF16, "Q")
        Ks = MIX.alloc([128, 4, SC], BF16, "K")
        VTs = MIX.alloc([128, 4, SC], BF16, "VT")
        Woutc = MIX.alloc([128, 4, 1024], BF16, "Woutc")
        MIXEND = MIX.off

        def phase_mixer(l, src, mid, dst):
            S.fence()
            stores = []
            bQ = [[Buf() for _ in range(8)] for _ in range(4)]
            bK = [[Buf() for _ in range(8)] for _ in range(4)]
            bV = [[Buf() for _ in range(8)] for _ in range(4)]
            bWout = Buf()
            bptail, bphalo, bgb0 = Buf(), Buf(), Buf()

            A = Arena(nc, MIXEND, PHLIM)
            Win = A.alloc([128, KC, 3072], BF16, "Win")
            xs = A.alloc([128, KC, TB], F32, "xsB")
            NRB = 4
            xr = [A.alloc([128, TB], F32, "xrB") for _ in range(NRB)]
            h1 = A.alloc([128, KC, TB], BF16, "h1")
            nt_off = A.off
            sq = [A.alloc([128, TB], BF16, "sqB") for _ in range(2)]
            mse = [A.alloc([128, TB], F32, "mseB") for _ in range(2)]
            rstd = [A.alloc([128, TB], F32, "rstdB") for _ in range(2)]
            tt = [A.alloc([128, TB], F32, "ttB") for _ in range(2)]
            nt_end = A.off
            pcarry = A.alloc([128, 4, 2], F32, "pcarry")
            gdummy = A.alloc([128, 8], F32, "gdummy")
            AX = Arena(nc, nt_off, nt_end)
            G = Buf("alias_guard")
            bgd = Buf()
            gcs = AX.alloc([128, TB], F32, "gcs")
            pbuf = AX.alloc([128, TB + 2], F32, "pbuf")
            y1 = AX.alloc([128, TB], F32, "y1")
            y2 = AX.alloc([128, TB], F32, "y2")
            yc = AX.alloc([128, 4, TB], BF16, "yc")
            bWin = [Buf() for _ in range(KC)]
            bxs = Buf()
            bxr = [Buf() for _ in range(NRB)]
            bh1 = [Buf() for _ in range(KC)]
            bsq = [Buf(), Buf()]
            bmse = [Buf(), Buf()]
            brstd = [Buf(), Buf()]
            btt = [Buf(), Buf()]
            bgcs, bpbuf, by1, by2 = Buf(), Buf(), Buf(), Buf()
            byc = [Buf() for _ in range(4)]
            bpcarry = Buf()

            def load_x(t):
                S.op("sp", lambda e, t=t: e.dma_start(out=xs[:].rearrange("p k c -> p (k c)"),
                                                      in_=src[t].rearrange("p k c -> p (k c)")),
                     writes=[bxs], dma=True)

            def guard_switch():
                S.op("pool", lambda e: e.memset(gdummy[:], 0.0), writes=[G, bgd])

            load_x(0)
            for k in range(KC):
                S.op("pool", lambda e, k=k: e.dma_start(out=Win[:, k, :], in_=win_d[l][:, k * 3072:(k + 1) * 3072]),
                     writes=[bWin[k]], dma=True)
            S.op("pool", lambda e: e.dma_start(out=Woutc[:].rearrange("p k c -> p (k c)"),
                                               in_=wout_d[l][:, 4 * 1024:8 * 1024]),
                 writes=[bWout], dma=True)
            S.op("dve", lambda e: e.memset(pcarry[:], 0.0), writes=[bpcarry])

            pbank = [ps[0], ps[1], ps[2], ps[3]]
            bpbank = [bps[0], bps[1], bps[2], bps[3]]
            pbc = [0]

            def proj(col0, T):
                bi = pbc[0] % 4
                pbc[0] += 1
                bank, bb = pbank[bi], bpbank[bi]
                for k in range(KC):
                    S.op("pe", lambda e, k=k, bank=bank, col0=col0: e.matmul(
                        bank[:, 0:T], lhsT=Win[:, k, col0:col0 + 128], rhs=h1[:, k, :],
                        start=(k == 0), stop=(k == KC - 1)), reads=[bWin[k], bh1[k]], writes=[bb])
                return bank, bb

            xrc = 0
            nrm = 0
            for t in range(NTB):
                c0 = t * TB
                blk = c0 // 512
                rms_affine(l, 1, xs, bxs, TB, h1, bh1, sq, bsq, mse[0], bmse[0], rstd[0], brstd[0], tt, btt,
                           ps[6], bps[6], xg=[G])
                if t + 1 < NTB:
                    load_x(t + 1)
                qk_pend = []
                for qk in range(2):
                    for c in range(4):
                        bank, bb = proj(qk * 512 + c * 128, TB)
                        s = nrm % 2
                        nrm += 1
                        S.op("act", lambda e, s=s, bank=bank: e.activation(out=sq[s][:], in_=bank[:, 0:TB],
                                                                            func=AF.Square),
                             reads=[bb, G], writes=[bsq[s]])

                        def rest(qk=qk, c=c, s=s, bank=bank, bb=bb, c0=c0, blk=blk):
                            msb, bmsb = ps[4 + s], bps[4 + s]
                            S.op("pe", lambda e: e.matmul(msb[:, 0:TB], lhsT=blockmean[:], rhs=sq[s][:],
                                                          start=True, stop=True),
                                 reads=[bsq[s], bconst, G], writes=[bmsb])
                            S.op("act", lambda e: e.activation(
                                out=mse[s][:], in_=msb[:, 0:TB], func=AF.Ln, bias=epsv[:, 0:1], scale=1.0),
                                reads=[bmsb, bconst, G], writes=[bmse[s]])
                            S.op("act", lambda e: e.activation(out=rstd[s][:], in_=mse[s][:], func=AF.Exp, scale=-0.5),
                                 reads=[bmse[s], G], writes=[brstd[s]])
                            dstT = Qs if qk == 0 else Ks
                            bd = (bQ if qk == 0 else bK)[c][blk]
                            gsc = gqs[:, l * 2:l * 2 + 1] if qk == 0 else gqk[:, l * 2 + 1:l * 2 + 2]
                            S.op("dve", lambda e: e.scalar_tensor_tensor(
                                out=dstT[:, c, c0:c0 + TB], in0=bank[:, 0:TB], scalar=gsc, in1=rstd[s][:],
                                op0=ALU.mult, op1=ALU.mult), reads=[bb, brstd[s], bvec, G], writes=[bd])

                        if qk_pend:
                            qk_pend.pop(0)()
                        qk_pend.append(rest)
                for c in range(4):
                    bank, bb = proj(1024 + c * 128, TB)
                    if qk_pend:
                        qk_pend.pop(0)()
                    S.op("act", lambda e, bank=bank, c=c, c0=c0: e.activation(out=VTs[:, c, c0:c0 + TB],
                                                                            in_=bank[:, 0:TB], func=AF.Copy),
                         reads=[bb], writes=[bV[c][blk]])
                guard_switch()
                for c in range(4):
                    bank, bb = proj(2048 + c * 128, TB)
                    S.op("act", lambda e, bank=bank: e.activation(out=gcs[:], in_=bank[:, 0:TB], func=AF.Copy),
                         reads=[bb, G], writes=[bgcs])
                    bank, bb = proj(2560 + c * 128, TB)
                    S.op("pool", lambda e, c=c: e.tensor_copy(out=pbuf[:, 0:2], in_=pcarry[:, c, :]),
                         reads=[bpcarry, G], writes=[bpbuf])
                    S.op("dve", lambda e, bank=bank: e.tensor_tensor(out=pbuf[:, 2:TB + 2], in0=bank[:, 0:TB],
                                                                     in1=gcs[:], op=ALU.mult),
                         reads=[bb, bgcs, G], writes=[bpbuf])
                    S.op("pool", lambda e, c=c: e.tensor_copy(out=pcarry[:, c, :], in_=pbuf[:, TB:TB + 2]),
                         reads=[bpbuf, G], writes=[bpcarry])
                    if t == NTB - 1:
                        S.op("pool", lambda e, c=c: e.tensor_copy(out=ptail[:, c, :], in_=pbuf[:, TB:TB + 2]),
                             reads=[bpbuf, G], writes=[bptail])
                    w0 = convw[:, l * 12 + c * 3 + 0:l * 12 + c * 3 + 1]
                    w1 = convw[:, l * 12 + c * 3 + 1:l * 12 + c * 3 + 2]
                    w2 = convw[:, l * 12 + c * 3 + 2:l * 12 + c * 3 + 3]
                    cb = convb[:, l * 4 + c:l * 4 + c + 1]
                    S.op("act", lambda e, w2=w2, cb=cb: e.activation(out=y1[:], in_=pbuf[:, 2:TB + 2], func=AF.Identity,
                                                                     scale=w2, bias=cb),
                         reads=[bpbuf, bvec, G], writes=[by1])
                    S.op("dve", lambda e, w1=w1: e.scalar_tensor_tensor(out=y2[:], in0=pbuf[:, 1:TB + 1], scalar=w1,
                                                                        in1=y1[:], op0=ALU.mult, op1=ALU.add),
                         reads=[bpbuf, by1, bvec, G], writes=[by2])
                    S.op("dve", lambda e, w0=w0: e.scalar_tensor_tensor(out=y1[:], in0=pbuf[:, 0:TB], scalar=w0,
                                                                        in1=y2[:], op0=ALU.mult, op1=ALU.add),
                         reads=[bpbuf, by2, bvec, G], writes=[by1])
                    bank, bb = proj(1536 + c * 128, TB)
                    S.op("dve", lambda e, bank=bank, c=c: e.tensor_tensor(out=yc[:, c, :], in0=bank[:, 0:TB], in1=y1[:],
                                                                          op=ALU.mult),
                         reads=[bb, by1, G], writes=[byc[c]])
                    if t == 0:
                        S.op("act", lambda e, bank=bank, c=c: e.activation(out=gb0[:, c, :], in_=bank[:, 0:2],
                                                                          func=AF.Copy),
                             reads=[bb], writes=[bgb0])
                def reloadB(t, d):
                    r = (t * KC + d) % NRB
                    S.op("sp", lambda e, t=t, d=d, r=r: e.dma_start(out=xr[r][:], in_=src[t][:, d, :]),
                         writes=[bxr[r]], dma=True)

                for d in range(NRB):
                    reloadB(t, d)
                for d in range(KC):
                    r = (t * KC + d) % NRB
                    ob, bob = ps[4 + d % 2], bps[4 + d % 2]
                    for c in range(4):
                        S.op("pe", lambda e, c=c, d=d, ob=ob: e.matmul(
                            ob[:, 0:TB], lhsT=Woutc[:, c, d * 128:(d + 1) * 128], rhs=yc[:, c, :],
                            start=(c == 0), stop=(c == 3)), reads=[bWout, byc[c], G], writes=[bob])
                    S.op("dve", lambda e, d=d, r=r, ob=ob: e.scalar_tensor_tensor(
                        out=xr[r][:], in0=ob[:, 0:TB], scalar=g_ap(l, 1, d), in1=xr[r][:], op0=ALU.mult, op1=ALU.add),
                        reads=[bob, bxr[r], bvec], writes=[bxr[r]])
                    stores.append(S.op("sp", lambda e, t=t, d=d, r=r: e.dma_start(
                        out=mid[t][:, d, :], in_=xr[r][:]), reads=[bxr[r]], dma=True))
                    if d + NRB < KC:
                        reloadB(t, d + NRB)
                guard_switch()

            tailreads = [bK[c][b] for c in range(4) for b in range(4, 8)]
            w_k = S.op("sp", lambda e: e.dma_start(
                out=ccin_d[l][0][:, :].rearrange("p (c t) -> p c t", c=4), in_=Ks[:, :, SC - HALO:SC]),
                reads=tailreads, dma=True)
            tailreads = [bV[c][b] for c in range(4) for b in range(4, 8)]
            w_v = S.op("sp", lambda e: e.dma_start(
                out=ccin_d[l][1][:, :].rearrange("p (c t) -> p c t", c=4), in_=VTs[:, :, SC - HALO:SC]),
                reads=tailreads, dma=True)
            w_p = S.op("sp", lambda e: e.dma_start(
                out=ccin_d[l][2][:, :],
                in_=ptail[:].rearrange("p c t -> p (c t)").bitcast(BF16)), reads=[bptail], dma=True)
            bccout = []
            for i, wop in enumerate((w_k, w_v, w_p)):
                bi_ = Buf()
                bi_.writer = wop
                bo_ = Buf()
                bccout.append(bo_)
                if no_cc:
                    S.op("sp", lambda e, i=i: e.dma_start(out=ccout_d[l][i][0:128, :], in_=ccin_d[l][i]),
                         reads=[bi_], writes=[bo_], dma=True)
                else:
                    S.op("pool", lambda e, i=i: e.collective_compute(
                        "AllGather", ALU.bypass, replica_groups=[[0, 1], [2, 3], [4, 5], [6, 7]],
                        ins=[ccin_d[l][i]], outs=[ccout_d[l][i]]), reads=[bi_], writes=[bo_], own_sem=True)
            if not no_cc:
                S.op("pool", lambda e: e.memset(ccdummy[:], 0.0), reads=list(bccout), writes=[bccdummy])

            A = Arena(nc, MIXEND, PHLIM)
            Wouta = A.alloc([128, 4, 1024], BF16, "Wouta")
            ycorr = A.alloc([128, 4, 2], BF16, "ycorr")
            dy = A.alloc([128, 4, 2], F32, "dy")
            DOVL = A.off
            Kh = A.alloc([128, 4, HALO], BF16, "Kh")
            Vh = A.alloc([128, 4, HALO], BF16, "Vh")
            Nacc = A.alloc([128, HALO], F32, "Nacc")
            Dacc = A.alloc([128, HALO], F32, "Dacc")
            Vtok = [A.alloc([128, 32, 128], BF16, "Vtok") for _ in range(2)]
            PT = [A.alloc([128, 512], BF16, "PT") for _ in range(4)]
            rD = [A.alloc([128, 512], F32, "rD") for _ in range(2)]
            bWouta = Buf()
            do_mod = defer_mod1 and l == 0
            if do_mod:
                cs2 = A.alloc([128, DM], F32, "cs2")
                wsl2 = [A.alloc([128, DM], F32, "wsl2") for _ in range(2)]
                mj2 = A.alloc([128, DM], BF16, "mj2")
                bcs2, bmj2 = Buf(), Buf()
                bwsl2 = [Buf(), Buf()]
            bKh, bVh = Buf(), Buf()
            bNacc = [Buf() for _ in range(4)]
            bDacc = [Buf() for _ in range(4)]
            bVtok = [[Buf() for _ in range(8)] for _ in range(2)]
            bPT = [Buf() for _ in range(4)]
            brD = [Buf(), Buf()]
            S.fence()
            S.op("pool", lambda e: e.dma_start(out=Wouta[:].rearrange("p k c -> p (k c)"),
                                               in_=wout_d[l][:, 0:4 * 1024]), writes=[bWouta], dma=True)
            S.op("sp", lambda e: e.dma_start(
                out=Kh[:], in_=ccout_d[l][0][0:128, :].rearrange("p (c t) -> p c t", c=4)),
                reads=[bccout[0]], writes=[bKh], dma=True)
            S.op("sp", lambda e: e.dma_start(
                out=Vh[:], in_=ccout_d[l][1][0:128, :].rearrange("p (c t) -> p c t", c=4)),
                reads=[bccout[1]], writes=[bVh], dma=True)
            S.op("sp", lambda e: e.dma_start(
                out=phalo[:].rearrange("p c t -> p (c t)").bitcast(BF16),
                in_=ccout_d[l][2][0:128, :]), reads=[bccout[2]], writes=[bphalo], dma=True)

            modq = list(range(72)) if do_mod else []

            def mod_dma(col):
                S.op("sp", lambda e, col=col: e.dma_start(out=wsl2[col % 2][:], in_=wada_d[1, col]),
                     writes=[bwsl2[col % 2]], dma=True)

            if do_mod:
                S.op("sp", lambda e: e.dma_start(out=cs2[:], in_=c_d), writes=[bcs2], dma=True)
                S.op("act", lambda e: e.activation(out=cs2[:], in_=cs2[:], func=AF.Silu), reads=[bcs2], writes=[bcs2])
                mod_dma(0)
                mod_dma(1)

            def mod_step():
                if not modq:
                    return
                col = modq.pop(0)
                S.op("dve", lambda e, col=col: e.scalar_tensor_tensor(
                    out=mj2[:], in0=wsl2[col % 2][:], scalar=1.0, in1=cs2[:], op0=ALU.mult, op1=ALU.mult,
                    accum_out=modraw[:, 72 + col:72 + col + 1]),
                    reads=[bwsl2[col % 2], bcs2], writes=[bmj2, bmodraw])
                if col + 2 < 72:
                    mod_dma(col + 2)
                if not modq:
                    derive(1)

            def kcols(d, r, b):
                start = 128 * d * b + r
                if b < 0:
                    return True, HALO + start, HALO + start + 127 * d + 1
                return False, start, start + 127 * d + 1

            def blks_of(bl, hp, c0, c1):
                return [bl[hp][x] for x in range(c0 // 512, (c1 - 1) // 512 + 1)]

            sbc = [0]
            ptc = [0]
            vtc = [0]
            ndc = [0]
            LOOK = 2
            pend = []

            def flush(keep_pv=0):
                while sum(1 for k, _ in pend if k == "pv") > keep_pv:
                    pend.pop(0)[1]()
                if keep_pv == 0:
                    while pend:
                        pend.pop(0)[1]()

            def emit_pv(qpair, kbidx, vs, hh, half, pi, Nb, bNb, Db, bDb):
                for jq, (r, b) in enumerate(qpair):
                    ocol = (2 * half + jq) * 128
                    for pc, kb in enumerate(((r, b - 1), (r, b))):
                        slot = kbidx[kb]
                        col = (2 * jq + pc) * 128
                        onesT = flag64 if kb[1] < 0 else ones64
                        S.op("pe", lambda e, slot=slot, vs=vs, hh=hh, col=col, ocol=ocol, pi=pi, Nb=Nb, pc=pc: e.matmul(
                            Nb[64 * hh:64 * hh + 64, ocol:ocol + 128], lhsT=Vtok[vs][:, slot, 64 * hh:64 * hh + 64],
                            rhs=PT[pi][:, col:col + 128], start=(pc == 0), stop=(pc == 1)),
                            reads=[bVtok[vs][slot // 4], bPT[pi]], writes=[bNb])
                        S.op("pe", lambda e, hh=hh, col=col, ocol=ocol, pi=pi, Db=Db, pc=pc, onesT=onesT: e.matmul(
                            Db[64 * hh:64 * hh + 64, ocol:ocol + 128], lhsT=onesT[:, :],
                            rhs=PT[pi][:, col:col + 128], start=(pc == 0), stop=(pc == 1)),
                            reads=[bconst, bPT[pi]], writes=[bDb])

            def emit_acc(d, bi, g, Nb, bNb, Db, bDb):
                if d == 1:
                    accN = Nacc[:, 512 * g:512 * g + 512]
                    accD = Dacc[:, 512 * g:512 * g + 512]
                    srcN, srcD = Nb[:, :], Db[:, :]
                    blks = [g]
                elif d == 4:
                    accN = Nacc[:, 512 * g:512 * g + 512].rearrange("p (i r) -> p r i", r=4)
                    accD = Dacc[:, 512 * g:512 * g + 512].rearrange("p (i r) -> p r i", r=4)
                    srcN = Nb[:, :].rearrange("p (r i) -> p r i", r=4)
                    srcD = Db[:, :].rearrange("p (r i) -> p r i", r=4)
                    blks = [g]
                else:
                    accN = Nacc[:, :].rearrange("p (i r) -> p r i", r=16)[:, 4 * g:4 * g + 4, :]
                    accD = Dacc[:, :].rearrange("p (i r) -> p r i", r=16)[:, 4 * g:4 * g + 4, :]
                    srcN = Nb[:, :].rearrange("p (r i) -> p r i", r=4)
                    srcD = Db[:, :].rearrange("p (r i) -> p r i", r=4)
                    blks = [0, 1, 2, 3]
                if bi == 0:
                    S.op("act", lambda e: e.activation(out=accN, in_=srcN, func=AF.Copy),
                         reads=[bNb], writes=[bNacc[x] for x in blks])
                    S.op("dve", lambda e: e.tensor_copy(out=accD, in_=srcD),
                         reads=[bDb], writes=[bDacc[x] for x in blks])
                else:
                    S.op("dve", lambda e: e.tensor_tensor(out=accN, in0=srcN, in1=accN, op=ALU.add),
                         reads=[bNb] + [bNacc[x] for x in blks], writes=[bNacc[x] for x in blks])
                    S.op("dve", lambda e: e.tensor_tensor(out=accD, in0=srcD, in1=accD, op=ALU.add),
                         reads=[bDb] + [bDacc[x] for x in blks], writes=[bDacc[x] for x in blks])

            for w in (1, 0):
                for hp in range(4):
                    for bi, d in enumerate(BRANCH_D):
                        nb_w = 16 // d
                        kbs = []
                        for r in range(d):
                            for b in range(w * nb_w - 1, (w + 1) * nb_w):
                                kbs.append((r, b))
                        kbs = [kb for kb in kbs if kb[1] < 0] + [kb for kb in kbs if kb[1] >= 0]
                        nhalo = sum(1 for kb in kbs if kb[1] < 0)
                        kbidx = {kb: n for n, kb in enumerate(kbs)}
                        vs = vtc[0] % 2
                        vtc[0] += 1
                        for n0 in range(0, len(kbs), 8):
                            grp = kbs[n0:n0 + 8]
                            for n, (r, b) in enumerate(grp):
                                ish, c0, c1 = kcols(d, r, b)
                                srcT = Vh if ish else VTs
                                rb = [bVh] if ish else blks_of(bV, hp, c0, c1)
                                S.op("pe", lambda e, n=n, srcT=srcT, c0=c0, c1=c1, d=d, hp=hp: e.transpose(
                                    out=psT[:, n * 128:(n + 1) * 128], in_=srcT[:, hp, c0:c1:d], identity=identbf[:]),
                                    reads=rb + [bconst], writes=[bpsT])
                            nh = max(0, min(len(grp), nhalo - n0))
                            wb = [bVtok[vs][x] for x in range(n0 // 4, (n0 + len(grp) - 1) // 4 + 1)]
                            if nh > 0:
                                S.op("dve", lambda e, n0=n0, nh=nh, vs=vs: e.tensor_scalar(
                                    out=Vtok[vs][:, n0:n0 + nh, :].rearrange("p s c -> p (s c)"),
                                    in0=psT[:, 0:nh * 128], scalar1=flag[:, 0:1], scalar2=None, op0=ALU.mult),
                                    reads=[bpsT, bvec], writes=wb)
                            if nh < len(grp):
                                S.op("act", lambda e, n0=n0, nh=nh, vs=vs, ng=len(grp): e.activation(
                                    out=Vtok[vs][:, n0 + nh:n0 + ng, :].rearrange("p s c -> p (s c)"),
                                    in_=psT[:, nh * 128:ng * 128], func=AF.Copy),
                                    reads=[bpsT], writes=wb)
                        if d == 1:
                            groups = [[(0, w * 16 + 4 * g + j) for j in range(4)] for g in range(4)]
                        elif d == 4:
                            groups = [[(r, w * 4 + g) for r in range(4)] for g in range(4)]
                        else:
                            groups = [[(4 * g + r, w) for r in range(4)] for g in range(4)]
                        for g, qbs in enumerate(groups):
                            mod_step()
                            nslot = ndc[0] % 2
                            ndc[0] += 1
                            Nb, bNb = ps[3 + nslot * 2], bps[3 + nslot * 2]
                            Db, bDb = ps[4 + nslot * 2], bps[4 + nslot * 2]
                            for hh in range(2):
                                rows = slice(64 * hh, 64 * hh + 64)
                                for half in range(2):
                                    qpair = qbs[2 * half:2 * half + 2]
                                    si = sbc[0] % 3
                                    sbc[0] += 1
                                    Sb, bSb = ps[si], bps[si]
                                    pi = ptc[0] % 4
                                    ptc[0] += 1
                                    for jq, (r, b) in enumerate(qpair):
                                        _, q0, q1 = kcols(d, r, b)
                                        for pc, kb in enumerate(((r, b - 1), (r, b))):
                                            ish, c0, c1 = kcols(d, kb[0], kb[1])
                                            ksrc = Kh if ish else Ks
                                            rb = [bKh] if ish else blks_of(bK, hp, c0, c1)
                                            rb = rb + blks_of(bQ, hp, q0, q1)
                                            col = (2 * jq + pc) * 128
                                            kw = dict(tile_position=(64, 0)) if hh == 1 else {}
                                            S.op("pe", lambda e, ksrc=ksrc, c0=c0, c1=c1, q0=q0, q1=q1, d=d, hp=hp,
                                                 rows=rows, col=col, Sb=Sb, kw=kw: e.matmul(
                                                Sb[:, col:col + 128], lhsT=ksrc[rows, hp, c0:c1:d],
                                                rhs=Qs[rows, hp, q0:q1:d], start=True, stop=True, **kw),
                                                reads=rb, writes=[bSb])
                                    S.op("act", lambda e, Sb=Sb, pi=pi: e.activation(out=PT[pi][:], in_=Sb[:, :],
                                                                                     func=AF.Exp),
                                         reads=[bSb], writes=[bPT[pi]])
                                    S.op("dve", lambda e, pi=pi: e.tensor_tensor(out=PT[pi][:], in0=PT[pi][:],
                                                                                 in1=maskbf[:], op=ALU.mult),
                                         reads=[bPT[pi], bconst], writes=[bPT[pi]])
                                    flush(keep_pv=LOOK - 1)
                                    pend.append(("pv", lambda qpair=qpair, kbidx=kbidx, vs=vs, hh=hh, half=half, pi=pi,
                                                 Nb=Nb, bNb=bNb, Db=Db, bDb=bDb:
                                                 emit_pv(qpair, kbidx, vs, hh, half, pi, Nb, bNb, Db, bDb)))
                            pend.append(("acc", lambda d=d, bi=bi, g=g, Nb=Nb, bNb=bNb, Db=Db, bDb=bDb:
                                         emit_acc(d, bi, g, Nb, bNb, Db, bDb)))
                    flush(0)
                    for x in range(4):
                        s = x % 2
                        S.op("act", lambda e, x=x, s=s: e.activation(out=rD[s][:], in_=Dacc[:, 512 * x:512 * x + 512],
                                                                     func=AF.Ln), reads=[bDacc[x]], writes=[brD[s]])
                        S.op("act", lambda e, s=s: e.activation(out=rD[s][:], in_=rD[s][:], func=AF.Exp, scale=-1.0),
                             reads=[brD[s]], writes=[brD[s]])
                        qblk = (w * HALO + 512 * x) // 512
                        S.op("dve", lambda e, x=x, s=s, hp=hp, w=w: e.tensor_tensor(
                            out=Qs[:, hp, w * HALO + 512 * x:w * HALO + 512 * x + 512],
                            in0=Nacc[:, 512 * x:512 * x + 512], in1=rD[s][:], op=ALU.mult),
                            reads=[bNacc[x], brD[s]], writes=[bQ[hp][qblk]])

            S.fence()
            AD = Arena(nc, DOVL, PHLIM)
            NXT = 3
            xt = [AD.alloc([128, KC, TF], F32, "xtD") for _ in range(NXT)]
            bxt = [[Buf() for _ in range(KC)] for _ in range(NXT)]
            bycorr, bdy = Buf(), Buf()
            for c in range(4):
                w0 = convw[:, l * 12 + c * 3 + 0:l * 12 + c * 3 + 1]
                w1 = convw[:, l * 12 + c * 3 + 1:l * 12 + c * 3 + 2]
                S.op("dve", lambda e, c=c, w0=w0: e.tensor_scalar(out=dy[:, c, :], in0=phalo[:, c, :], scalar1=w0,
                                                                 scalar2=None, op0=ALU.mult),
                     reads=[bphalo, bvec], writes=[bdy])
                S.op("dve", lambda e, c=c, w1=w1: e.scalar_tensor_tensor(
                    out=dy[:, c, 0:1], in0=phalo[:, c, 1:2], scalar=w1, in1=dy[:, c, 0:1], op0=ALU.mult, op1=ALU.add),
                    reads=[bphalo, bdy, bvec], writes=[bdy])
            S.op("dve", lambda e: e.scalar_tensor_tensor(
                out=ycorr[:].rearrange("p c t -> p (c t)"), in0=dy[:].rearrange("p c t -> p (c t)"),
                scalar=flag[:, 0:1], in1=gb0[:].rearrange("p c t -> p (c t)"), op0=ALU.mult, op1=ALU.mult),
                reads=[bdy, bgb0, bvec], writes=[bycorr])

            def loadD(t):
                S.op("sp", lambda e, t=t: e.dma_start(out=xt[t % NXT][:].rearrange("p k c -> p (k c)"),
                                                      in_=mid[t].rearrange("p k c -> p (k c)")),
                     writes=bxt[t % NXT], dma=True)

            loadD(0)
            loadD(1)
            for t in range(NTF):
                sl = t % NXT
                for d in range(KC):
                    ob, bob = ps[d % 3], bps[d % 3]
                    for c in range(4):
                        S.op("pe", lambda e, c=c, d=d, ob=ob, t=t: e.matmul(
                            ob[:, :], lhsT=Wouta[:, c, d * 128:(d + 1) * 128], rhs=Qs[:, c, t * TF:(t + 1) * TF],
                            start=(c == 0), stop=(c == 3 and t != 0)), reads=[bWouta, bQ[c][t]], writes=[bob])
                    if t == 0:
                        for c in range(4):
                            S.op("pe", lambda e, c=c, d=d, ob=ob: e.matmul(
                                ob[:, 0:2], lhsT=Woutc[:, c, d * 128:(d + 1) * 128], rhs=ycorr[:, c, :],
                                start=False, stop=(c == 3)), reads=[bWout, bycorr], writes=[bob])
                    S.op("dve", lambda e, d=d, sl=sl, ob=ob: e.scalar_tensor_tensor(
                        out=xt[sl][:, d, :], in0=ob[:, :], scalar=g_ap(l, 1, d), in1=xt[sl][:, d, :],
                        op0=ALU.mult, op1=ALU.add), reads=[bob, bxt[sl][d], bvec], writes=[bxt[sl][d]])
                stores.append(S.op("sp", lambda e, t=t, sl=sl: e.dma_start(
                    out=dst[t].rearrange("p k c -> p (k c)"), in_=xt[sl][:].rearrange("p k c -> p (k c)")),
                    reads=bxt[sl], dma=True))
                if t + 2 < NTF:
                    loadD(t + 2)
            return stores

        plan = []
        for l in range(n_layers):
            plan += [("ffn", l, 0), ("mix", l, None), ("ffn", l, 1)]
        if stop_after is not None:
            plan = plan[:stop_after]
        if mixer_only:
            plan = [("mix", 0, None)]
        final = []
        for n, (kind, l, f) in enumerate(plan):
            src = x_d if n == 0 else xs_d
            dst = y_d if n == len(plan) - 1 else xs_d
            if kind == "ffn":
                final = phase_ffn(l, f, src, dst)
            else:
                final = phase_mixer(l, src, xs_d if mixer_only else src, dst)
        S.run_block(nc, st, final_waits=final)
    return nc


_CACHE = {}


def _host_layouts(x, c, w_ada, b_ada, norm_g, w_in, q_norm_g, k_norm_g, conv_w, conv_b, w_out, ffn_w1, ffn_w2):
    f = np.float32
    asc = np.ascontiguousarray
    shared = {}
    shared["wada"] = asc(w_ada.reshape(2, DM, 72, 128).transpose(0, 2, 3, 1))
    shared["bada"] = asc(b_ada.reshape(2, 72, 128).transpose(2, 0, 1).reshape(128, 144))
    shared["normg"] = asc(norm_g.reshape(2, 3, KC, 128).transpose(3, 0, 1, 2).reshape(128, 48))
    gq = np.concatenate([q_norm_g, q_norm_g], axis=1)
    gk = np.concatenate([k_norm_g, k_norm_g], axis=1)
    shared["gqk"] = asc(np.stack([gq[0], gk[0], gq[1], gk[1]], axis=1))
    shared["convw"] = asc(conv_w.reshape(2, 3, 4, 128).transpose(3, 0, 2, 1).reshape(128, 24))
    shared["convb"] = asc(conv_b.reshape(2, 4, 128).transpose(2, 0, 1).reshape(128, 8))
    k = np.arange(128)[:, None]
    q = np.arange(128)[None, :]
    prev = (k >= q).astype(f)
    cur = (k <= q).astype(f)
    shared["mask"] = asc(np.concatenate([prev, cur, prev, cur], axis=1))
    w1 = ffn_w1.reshape(2, 2, KC, 128, 2, NG1, 256)
    shared["w1"] = asc(w1.transpose(0, 1, 5, 3, 2, 4, 6)).reshape(2, 2, NG1, 128, KC * 512)
    shared["w2"] = asc(ffn_w2.reshape(2, 2, JC, 128, 1024).transpose(0, 1, 3, 2, 4)).reshape(2, 2, 128, JC * 1024)
    shared["win"] = asc(w_in.reshape(2, KC, 128, 3072).transpose(0, 2, 1, 3)).reshape(2, 128, KC * 3072)
    shared["wout"] = asc(w_out.reshape(2, KC, 128, 1024).transpose(0, 2, 1, 3)).reshape(2, 128, KC * 1024)
    in_maps = []
    for core in range(NCORES):
        b, half = core // 2, core % 2
        xc = x[b, half * SC:(half + 1) * SC]
        xfm = asc(xc.reshape(NTF, TF, KC, 128).transpose(0, 3, 2, 1))
        m = dict(shared)
        m["x_fm"] = xfm
        m["c_fm"] = asc(np.broadcast_to(c[b][None, :], (128, DM)))
        m["flag"] = np.full((128, 1), float(half), dtype=f)
        in_maps.append(m)
    return in_maps


def kernel(x, c, w_ada, b_ada, norm_g, w_in, q_norm_g, k_norm_g, conv_w, conv_b, w_out, ffn_w1, ffn_w2,
           _stop_after=None, _mixer_only=False, _no_cc=False):
    args = [np.asarray(a, dtype=np.float32) for a in
            (x, c, w_ada, b_ada, norm_g, w_in, q_norm_g, k_norm_g, conv_w, conv_b, w_out, ffn_w1, ffn_w2)]
    in_maps = _host_layouts(*args)
    key = ("nc", _stop_after, _mixer_only, _no_cc)
    if key not in _CACHE:
        _CACHE[key] = build_program(stop_after=_stop_after, mixer_only=_mixer_only, no_cc=_no_cc)
    if _mixer_only:
        for m in in_maps:
            m["wada"] = np.zeros((1, 1, 128, 16), np.float32)
            m["w1"] = np.zeros((1, 1, 1, 128, 16), np.float32)
            m["w2"] = np.zeros((1, 1, 128, 16), np.float32)
    nc = _CACHE[key]
    res = run_bass_kernel_spmd(nc, in_maps, core_ids=list(range(NCORES)))
    out = np.empty((4, 8192, DM), dtype=np.float32)
    for core in range(NCORES):
        b, half = core // 2, core % 2
        yfm = np.asarray(res.results[core]["y_fm"])
        out[b, half * SC:(half + 1) * SC] = yfm.transpose(0, 3, 2, 1).reshape(SC, DM)
    return out
```
